# Optimizing a Trainium2 kernel written in Bass

```python
import math
import jax, jax.numpy as jnp
from jax import lax
import numpy as np

D_MODEL = 1024
BATCH = 4
SEQ = 8192
DEPTH = 2
DEC_BATCH = 1
DEC_SEQ = 16384
PAST_LEN = 128

GRID_W = 64
HEAD_DIM = 128
A_GROUPS = ((128, 1), (512, 4), (2048, 16))
A_N_GROUPS = 3
A_HEADS_PER_GROUP = 4
A_HEADS = A_N_GROUPS * A_HEADS_PER_GROUP
A_QKV_WIDTH = A_HEADS * HEAD_DIM
A_WIDTH = A_HEADS_PER_GROUP * HEAD_DIM
ALIBI_MAX_BIAS = 8.0
B_HEADS = 8
B_KV_HEADS = 2
B_WIDTH = B_HEADS * HEAD_DIM
B_KV_WIDTH = B_KV_HEADS * HEAD_DIM
Q_BLOCK = 128
ROPE_THETA = 10000.0
GATE_WIDTH = 2 * D_MODEL
IN_WIDTHS = (A_QKV_WIDTH, A_QKV_WIDTH, A_QKV_WIDTH, A_WIDTH,
             B_WIDTH, B_KV_WIDTH, B_KV_WIDTH, B_WIDTH, D_MODEL, D_MODEL)
IN_WIDTH = 3 * A_QKV_WIDTH + A_WIDTH + 2 * B_WIDTH + 2 * B_KV_WIDTH + GATE_WIDTH
RMS_EPS = 1e-6
LN_EPS = 1e-5
MASK_VALUE = -1e30
ALPHA = (2 * DEPTH) ** 0.25
BETA = (8 * DEPTH) ** -0.25

kernel_name = "hybrid_dilated_gqa_encoder"


def layer_norm(x, g, b):
    xf = x.astype(jnp.float32)
    mu = jnp.mean(xf, axis=-1, keepdims=True)
    var = jnp.mean(jnp.square(xf - mu), axis=-1, keepdims=True)
    y = (xf - mu) * lax.rsqrt(var + LN_EPS) * g.astype(jnp.float32) + b.astype(jnp.float32)
    return y.astype(x.dtype)


def rms_norm(x, g):
    xf = x.astype(jnp.float32)
    y = xf * lax.rsqrt(jnp.mean(jnp.square(xf), axis=-1, keepdims=True) + RMS_EPS) * g.astype(jnp.float32)
    return y.astype(x.dtype)


def alibi_slopes():
    return jnp.asarray(2.0 ** (-ALIBI_MAX_BIAS * np.arange(1, A_HEADS + 1) / A_HEADS), dtype=jnp.float32)


def banded_attention(q, k, v, slopes, dil, n_side):
    N, L, H, Dh = q.shape
    blk = n_side
    nb = -(-L // blk)
    Lp = nb * blk
    qb = jnp.pad(q, ((0, 0), (0, Lp - L), (0, 0), (0, 0))).reshape(N, nb, blk, H, Dh)
    pad_kv = ((0, 0), (blk, Lp - L + blk), (0, 0), (0, 0))

    def neighbourhood(t):
        tb = jnp.pad(t, pad_kv).reshape(N, nb + 2, blk, H, Dh)
        return jnp.concatenate([tb[:, :-2], tb[:, 1:-1], tb[:, 2:]], axis=2)

    kb = neighbourhood(k)
    vb = neighbourhood(v)
    rel = jnp.arange(3 * blk)[None, :] - blk - jnp.arange(blk)[:, None]
    key_pos = jnp.arange(nb)[:, None] * blk - blk + jnp.arange(3 * blk)[None, :]
    valid = (jnp.abs(rel) <= n_side)[None] & ((key_pos >= 0) & (key_pos < L))[:, None, :]
    alibi = -slopes[:, None, None] * (dil * jnp.abs(rel)).astype(jnp.float32)[None]
    scale = 1.0 / math.sqrt(Dh)
    s = jnp.einsum('nbqhd,nbkhd->nbhqk', qb, kb).astype(jnp.float32) * scale + alibi[None, None]
    s = jnp.where(valid[None, :, None], s, MASK_VALUE)
    m = jnp.max(s, axis=-1, keepdims=True)
    p = jnp.exp(s - m)
    l = jnp.sum(p, axis=-1, keepdims=True)
    o = jnp.einsum('nbhqk,nbkhd->nbqhd', (p / l).astype(v.dtype), vb)
    lse = (m + jnp.log(l))[..., 0]
    o = o.reshape(N, Lp, H, Dh)[:, :L]
    lse = lse.transpose(0, 1, 3, 2).reshape(N, Lp, H)[:, :L]
    return o, lse


def dilated_group(q, k, v, slopes, dil, n_side):
    B, S, H, Dh = q.shape
    L = S // dil

    def to_classes(t):
        return t.reshape(B, L, dil, H, Dh).transpose(0, 2, 1, 3, 4).reshape(B * dil, L, H, Dh)

    o, lse = banded_attention(to_classes(q), to_classes(k), to_classes(v), slopes, dil, n_side)
    o = o.reshape(B, dil, L, H, Dh).transpose(0, 2, 1, 3, 4).reshape(B, S, H, Dh)
    lse = lse.reshape(B, dil, L, H).transpose(0, 2, 1, 3).reshape(B, S, H)
    return o, lse


def dilated_mixer(q, k, v, slopes):
    B, S, _ = q.shape
    shp = (B, S, A_N_GROUPS, A_HEADS_PER_GROUP, HEAD_DIM)
    q, k, v = q.reshape(shp), k.reshape(shp), v.reshape(shp)
    slopes = slopes.reshape(A_N_GROUPS, A_HEADS_PER_GROUP)
    outs, lses = [], []
    for g, (window, dil) in enumerate(A_GROUPS):
        n_side = window // (2 * dil)
        o, lse = dilated_group(q[:, :, g], k[:, :, g], v[:, :, g], slopes[g], dil, n_side)
        outs.append(o)
        lses.append(lse)
    o = jnp.stack(outs)
    w = jax.nn.softmax(jnp.stack(lses), axis=0)
    out = jnp.einsum('gbsh,gbshd->bshd', w.astype(o.dtype), o)
    return out.reshape(B, S, A_WIDTH)


def axial_rope_tables(S):
    rows = S // GRID_W
    row_pos = jnp.repeat(jnp.arange(rows), GRID_W).astype(jnp.float32)
    col_pos = jnp.tile(jnp.arange(GRID_W), rows).astype(jnp.float32)
    axis_dim = HEAD_DIM // 2
    inv_freq = ROPE_THETA ** (-jnp.arange(0, axis_dim, 2, dtype=jnp.float32) / axis_dim)
    ang_r = row_pos[:, None] * inv_freq[None]
    ang_c = col_pos[:, None] * inv_freq[None]
    ang = jnp.concatenate([ang_r, ang_r, ang_c, ang_c], axis=-1)
    return jnp.cos(ang), jnp.sin(ang)


def apply_axial_rope(x, cos, sin):
    xf = x.astype(jnp.float32)
    xs = xf.reshape(*x.shape[:-1], 2, 2, HEAD_DIM // 4)
    rot = jnp.stack([-xs[..., 1, :], xs[..., 0, :]], axis=-2).reshape(x.shape)
    return (xf * cos[None, :, None, :] + rot * sin[None, :, None, :]).astype(x.dtype)


def gqa_mixer(q, k, v, q_gain, k_gain):
    B, S, _ = q.shape
    G = B_HEADS // B_KV_HEADS
    q = rms_norm(q.reshape(B, S, B_HEADS, HEAD_DIM), q_gain)
    k = rms_norm(k.reshape(B, S, B_KV_HEADS, HEAD_DIM), k_gain)
    v = v.reshape(B, S, B_KV_HEADS, HEAD_DIM)
    cos, sin = axial_rope_tables(S)
    q = apply_axial_rope(q, cos, sin)
    k = apply_axial_rope(k, cos, sin)
    scale = 1.0 / math.sqrt(HEAD_DIM)
    qb = q.reshape(B, S // Q_BLOCK, Q_BLOCK, B_KV_HEADS, G, HEAD_DIM).transpose(1, 0, 2, 3, 4, 5)

    def attend_block(qi):
        s = jnp.einsum('bqkgd,bskd->bkgqs', qi, k).astype(jnp.float32) * scale
        p = jax.nn.softmax(s, axis=-1).astype(v.dtype)
        return jnp.einsum('bkgqs,bskd->bqkgd', p, v)

    o = lax.map(attend_block, qb)
    return o.transpose(1, 0, 2, 3, 4, 5).reshape(B, S, B_WIDTH)


def encoder_layer(x, w_in, b_in, q_gain, k_gain, w_proj_a, w_proj_b, w_out, ln_g, ln_b, slopes):
    h = jnp.einsum('bsd,de->bse', x, w_in) + b_in
    splits = [int(i) for i in np.cumsum(IN_WIDTHS)[:-1]]
    aq, ak, av, ag, bq, bk, bv, bg, gate_a, gate_b = jnp.split(h, splits, axis=-1)
    ya = dilated_mixer(aq, ak, av, slopes) * jax.nn.silu(ag)
    yb = gqa_mixer(bq, bk, bv, q_gain, k_gain) * jax.nn.silu(bg)
    merged = (jax.nn.sigmoid(gate_a) * jnp.einsum('bse,ed->bsd', ya, w_proj_a)
              + jax.nn.sigmoid(gate_b) * jnp.einsum('bse,ed->bsd', yb, w_proj_b))
    sub = jnp.einsum('bsd,de->bse', merged, w_out)
    return layer_norm(ALPHA * x + sub, ln_g, ln_b)


def run_trunk(x, w_in, b_in, q_gain, k_gain, w_proj_a, w_proj_b, w_out, ln_g, ln_b):
    slopes = alibi_slopes()
    for l in range(DEPTH):
        x = encoder_layer(x, w_in[l], b_in[l], q_gain[l], k_gain[l], w_proj_a[l], w_proj_b[l],
                          w_out[l], ln_g[l], ln_b[l], slopes)
    return x


def setup_inputs(seed: int = 0) -> dict:
    key = jax.random.key(seed)
    ks = jax.random.split(key, 12)
    f32 = jnp.float32
    x_prompt = jax.random.normal(ks[0], (BATCH, SEQ, D_MODEL), f32)
    x_sample = jax.random.normal(ks[1], (DEC_BATCH, DEC_SEQ, D_MODEL), f32)
    w_in = jax.random.normal(ks[2], (DEPTH, D_MODEL, IN_WIDTH), f32) * D_MODEL ** -0.5
    b_in = jax.random.normal(ks[3], (DEPTH, IN_WIDTH), f32) * 0.02
    q_gain = 1.0 + 0.05 * jax.random.normal(ks[4], (DEPTH, HEAD_DIM), f32)
    k_gain = 1.0 + 0.05 * jax.random.normal(ks[5], (DEPTH, HEAD_DIM), f32)
    w_proj_a = jax.random.normal(ks[6], (DEPTH, A_WIDTH, D_MODEL), f32) * (A_WIDTH ** -0.5 * BETA)
    w_proj_b = jax.random.normal(ks[7], (DEPTH, B_WIDTH, D_MODEL), f32) * (B_WIDTH ** -0.5 * BETA)
    w_out = jax.random.normal(ks[8], (DEPTH, D_MODEL, D_MODEL), f32) * (D_MODEL ** -0.5 * BETA)
    ln_g = 1.0 + 0.05 * jax.random.normal(ks[9], (DEPTH, D_MODEL), f32)
    ln_b = 0.02 * jax.random.normal(ks[10], (DEPTH, D_MODEL), f32)
    return {"x_prompt": x_prompt, "x_sample": x_sample, "w_in": w_in, "b_in": b_in,
            "q_gain": q_gain, "k_gain": k_gain, "w_proj_a": w_proj_a, "w_proj_b": w_proj_b,
            "w_out": w_out, "ln_g": ln_g, "ln_b": ln_b}


def reference(x_prompt, x_sample, w_in, b_in, q_gain, k_gain, w_proj_a, w_proj_b, w_out, ln_g, ln_b):
    y_prompt = run_trunk(x_prompt, w_in, b_in, q_gain, k_gain, w_proj_a, w_proj_b, w_out, ln_g, ln_b)
    y_sample = run_trunk(x_sample, w_in, b_in, q_gain, k_gain, w_proj_a, w_proj_b, w_out, ln_g, ln_b)
    return (y_prompt, y_sample)
```

```python
import contextlib
import math
import numpy as np
import ml_dtypes
import concourse.bass as bass
import concourse.mybir as mybir
from concourse.bass_utils import run_bass_kernel_spmd

F32 = mybir.dt.float32
BF16 = mybir.dt.bfloat16
AF = mybir.ActivationFunctionType
ALU = mybir.AluOpType

D = 1024
HD = 128
INW = 9728
OFF_AQ, OFF_AK, OFF_AV, OFF_AG = 0, 1536, 3072, 4608
OFF_BQ, OFF_BK, OFF_BV, OFF_BG = 5120, 6144, 6400, 6656
OFF_GA, OFF_GB = 7680, 8704
DEPTH = 2
GRID_W = 64
A_GROUPS = ((128, 1), (512, 4), (2048, 16))
ALPHA = (2 * DEPTH) ** 0.25
RMS_EPS = 1e-6
LN_EPS = 1e-5
SCALE = 1.0 / math.sqrt(HD)
TQ = 512
PADL = 1024
PADR = 1024
NEG = -700.0
NDMA = 8


class Tl:
    __slots__ = ("t", "lw", "rd")

    def __init__(self, t):
        self.t = t
        self.lw = None
        self.rd = {}


class Sched:
    ENGS = ("pe", "act", "dve", "pool", "sp")

    def __init__(self):
        self.items = {e: [] for e in self.ENGS}
        self.cnt = {e: 0 for e in ("pe", "act", "dve", "pool")}
        self.dma_n = {"sp": 0, "pool": 0, "act": 0}
        self.dma_val = {}
        self.seen = {e: {} for e in self.ENGS}

    def _deps(self, eng, reads, writes):
        deps = {}

        def add(k, v):
            if deps.get(k, 0) < v:
                deps[k] = v
        for t in reads:
            if t.lw is not None:
                add(*t.lw)
        for t in writes:
            if t.lw is not None:
                add(*t.lw)
            for k, v in t.rd.items():
                add(k, v)
        waits = []
        for k, v in deps.items():
            if k == "pe" and eng == "pe":
                continue
            if self.seen[eng].get(k, 0) >= v:
                continue
            self.seen[eng][k] = v
            waits.append((k, v))
        return waits

    def _mark(self, me, reads, writes):
        k, v = me
        for t in reads:
            if t.rd.get(k, 0) < v:
                t.rd[k] = v
        for t in writes:
            t.lw = me
            t.rd = {}

    def op(self, eng, emit, reads=(), writes=()):
        waits = self._deps(eng, reads, writes)
        self.cnt[eng] += 1
        me = (eng, self.cnt[eng])
        self._mark(me, reads, writes)
        self.items[eng].append((waits, emit, eng, 1))

    def dma(self, q, out, in_, reads=(), writes=(), slow=False):
        n = self.dma_n[q]
        self.dma_n[q] = n + 1
        key = ("dma", q, n % NDMA)
        prev = self.dma_val.get(key, 0)
        waits = self._deps(q, reads, writes)
        if prev and self.seen[q].get(key, 0) < prev:
            self.seen[q][key] = prev
            waits.append((key, prev))
        val = prev + 16
        self.dma_val[key] = val
        self._mark((key, val), reads, writes)
        if slow:
            emit = lambda e: e.dma_start(out=out, in_=in_, allow_slow_non_contiguous=True)
        else:
            emit = lambda e: e.dma_start(out=out, in_=in_)
        self.items[q].append((waits, emit, key, 16))

    def barrier(self):
        allv = dict(self.cnt)
        allv.update(self.dma_val)
        for e in self.ENGS:
            waits = []
            for k, v in allv.items():
                if v and k != e and self.seen[e].get(k, 0) < v:
                    self.seen[e][k] = v
                    waits.append((k, v))
            if waits:
                self.items[e].append((waits, None, None, 0))


def build(HALF):
    N = 2 * HALF
    NT = N // TQ
    TPH = HALF // TQ
    nc = bass.Bass("TRN2", target_bir_lowering=False)
    S = Sched()
    es = contextlib.ExitStack()

    def dram(name, shape, dt, kind="Internal"):
        return nc.dram_tensor(name, list(shape), dt, kind=kind)

    xT = dram("xT", [D, N], F32, "ExternalInput")
    w_in = dram("w_in", [DEPTH, D, INW], F32, "ExternalInput")
    b_in = dram("b_in", [DEPTH, INW], F32, "ExternalInput")
    q_gain = dram("q_gain", [DEPTH, HD], F32, "ExternalInput")
    k_gain = dram("k_gain", [DEPTH, HD], F32, "ExternalInput")
    w_pa = dram("w_proj_a", [DEPTH, 512, D], F32, "ExternalInput")
    w_pb = dram("w_proj_b", [DEPTH, D, D], F32, "ExternalInput")
    w_o = dram("w_out", [DEPTH, D, D], F32, "ExternalInput")
    ln_g = dram("ln_g", [DEPTH, D], F32, "ExternalInput")
    ln_b = dram("ln_b", [DEPTH, D], F32, "ExternalInput")
    cosT = dram("cosT", [HD, N], F32, "ExternalInput")
    sinT = dram("sinT", [HD, N], F32, "ExternalInput")
    permM = dram("permM", [HD, HD], F32, "ExternalInput")
    cross = dram("cross", [128, 1], F32, "ExternalInput")
    NVAR = 9
    etab = dram("etab", [NVAR, 3, 4, 2, 128, TQ], BF16, "ExternalInput")
    yT = dram("yT", [D, N], F32, "ExternalOutput")
    wbf = dram("wbf", [DEPTH, INW // 128, 128, 8 * 128], BF16)
    wpabf = dram("wpabf", [DEPTH, 128, 4 * D], BF16)
    wpbbf = dram("wpbbf", [DEPTH, 128, 8 * D], BF16)
    wobf = dram("wobf", [DEPTH, 128, 8 * D], BF16)
    x1T = dram("x1T", [D, N], F32)
    NPAD = PADL + N + PADR
    akT = dram("akT", [12, 128, NPAD], BF16)
    av = dram("av", [NPAD, 1536], BF16)
    kbT = dram("kbT", [2, 128, N], BF16)
    vb = dram("vb", [N, 256], BF16)

    def sb(name, shape, dt):
        return Tl(es.enter_context(nc.sbuf_tensor(name, list(shape), dt)))

    def psum(name):
        return Tl(es.enter_context(nc.psum_tensor(name, [128, TQ], F32)))

    ones_bf = sb("ones_bf", [128, 512], BF16)
    zero_bf = sb("zero_bf", [128, 1536], BF16)
    perm_f = sb("perm_f", [128, 128], F32)
    perm_b = sb("perm_b", [128, 128], BF16)
    cross_t = sb("cross_t", [128, 1], F32)
    cross_bf = sb("cross_bf", [128, 1], BF16)
    eps_rms = sb("eps_rms", [128, 1], F32)
    eps_ln = sb("eps_ln", [128, 1], F32)
    bias_fm = sb("bias_fm", [128, INW // 128], F32)
    bias_row = sb("bias_row", [1, 1792], BF16)
    qg_t = sb("qg_t", [128, 1], F32)
    kg_t = sb("kg_t", [128, 1], F32)
    lng_t = sb("lng_t", [128, 8], F32)
    lnb_t = sb("lnb_t", [128, 8], F32)
    xf = [sb(f"xf{i}", [128, TQ], F32) for i in range(2)]
    xb = sb("xb", [128, 8, TQ], BF16)
    wstage = [sb(f"wstage{i}", [128, 8, 128], F32) for i in range(1)]
    wt = [sb(f"wt{i}", [128, 8, 128], BF16) for i in range(3)]
    wpa_t = sb("wpa_t", [128, 4, D], BF16)
    wpb_t = sb("wpb_t", [128, 8, D], BF16)
    wo_t = sb("wo_t", [128, 8, D], BF16)
    cos_t = sb("cos_t", [128, TQ], F32)
    sin_t = sb("sin_t", [128, TQ], F32)
    tA = sb("tA", [128, TQ], F32)
    tB = sb("tB", [128, TQ], F32)
    tC = sb("tC", [128, TQ], F32)
    tD = sb("tD", [128, TQ], F32)
    tbf = sb("tbf", [128, TQ], BF16)
    tbf2 = sb("tbf2", [128, TQ], BF16)
    stg = [sb(f"stg{i}", [128, TQ], BF16) for i in range(2)]
    qb = [sb(f"qb{i}", [128, TQ], BF16) for i in range(4)]
    gsil = sb("gsil", [128, TQ], F32)
    yb = sb("yb", [128, 8, TQ], BF16)
    ya = sb("ya", [128, 4, TQ], BF16)
    Pt = [sb(f"Pt{i}", [128, TQ], BF16) for i in range(3)]
    Pm = [sb(f"Pm{i}", [128, TQ], BF16) for i in range(2)]
    Et = [sb(f"Et{i}", [128, TQ], BF16) for i in range(2)]
    Kg = [sb(f"Kg{i}", [128, 2048], BF16) for i in range(2)]
    Vg = [sb(f"Vg{i}", [128, 16, 128], BF16) for i in range(2)]
    Kw = [sb(f"Kw{i}", [128, 2560], BF16) for i in range(2)]
    Vw = [sb(f"Vw{i}", [128, 32, 128], BF16) for i in range(2)]
    qa = [sb(f"qa{i}", [128, TQ], BF16) for i in range(2)]
    lrow = sb("lrow", [128, TQ], F32)
    lhi = sb("lhi", [128, TQ], BF16)
    llo = sb("llo", [128, TQ], BF16)
    lbc = sb("lbc", [128, TQ], F32)
    mrg = sb("mrg", [128, 8, TQ], BF16)
    z = sb("z", [128, 8, TQ], F32)
    yout = [sb(f"yout{i}", [128, TQ], F32) for i in range(2)]
    ps = [psum(f"ps{i}") for i in range(8)]
    PS_S = [ps[0], ps[1]]
    PS_O = [ps[2], ps[3], ps[4], ps[5]]
    PS_L = ps[6]
    PS_G = ps[7]

    def mm(out_t, out_ap, lhs_t, lhs_ap, rhs_t, rhs_ap, start, stop):
        S.op("pe", lambda e: e.matmul(out_ap, lhsT=lhs_ap, rhs=rhs_ap, start=start, stop=stop,
                                      skip_group_check=True),
             reads=[lhs_t, rhs_t] + ([] if start else [out_t]), writes=[out_t])

    def act(out_t, out_ap, in_t, in_ap, func, bias=0.0, scale=1.0, extra_reads=()):
        S.op("act", lambda e: e.activation(out=out_ap, in_=in_ap, func=func, bias=bias, scale=scale),
             reads=[in_t] + list(extra_reads), writes=[out_t])

    def tt(eng, out_t, out_ap, a_t, a_ap, b_t, b_ap, op):
        S.op(eng, lambda e: e.tensor_tensor(out=out_ap, in0=a_ap, in1=b_ap, op=op),
             reads=[a_t, b_t], writes=[out_t])

    def ts(eng, out_t, out_ap, a_t, a_ap, s1, s2, op0, op1, extra_reads=()):
        if s2 is None:
            S.op(eng, lambda e: e.tensor_scalar(out=out_ap, in0=a_ap, scalar1=s1, scalar2=None, op0=op0),
                 reads=[a_t] + list(extra_reads), writes=[out_t])
        else:
            S.op(eng, lambda e: e.tensor_scalar(out=out_ap, in0=a_ap, scalar1=s1, scalar2=s2, op0=op0, op1=op1),
                 reads=[a_t] + list(extra_reads), writes=[out_t])

    def stt(eng, out_t, out_ap, a_t, a_ap, sc, b_t, b_ap, op0, op1, extra_reads=()):
        S.op(eng, lambda e: e.scalar_tensor_tensor(out=out_ap, in0=a_ap, scalar=sc, in1=b_ap, op0=op0, op1=op1),
             reads=[a_t, b_t] + list(extra_reads), writes=[out_t])

    def cp(eng, out_t, out_ap, in_t, in_ap):
        S.op(eng, lambda e: e.tensor_copy(out=out_ap, in_=in_ap), reads=[in_t], writes=[out_t])

    def mset(eng, t, ap, val):
        S.op(eng, lambda e: e.memset(ap, val), writes=[t])

    mset("dve", ones_bf, ones_bf.t[:], 1.0)
    mset("dve", zero_bf, zero_bf.t[:], 0.0)
    mset("dve", eps_rms, eps_rms.t[:], RMS_EPS)
    mset("dve", eps_ln, eps_ln.t[:], LN_EPS)
    S.dma("sp", perm_f.t[:], permM[:, :], writes=[perm_f])
    cp("dve", perm_b, perm_b.t[:], perm_f, perm_f.t[:])
    S.dma("sp", cross_t.t[:], cross[:, :], writes=[cross_t])
    cp("dve", cross_bf, cross_bf.t[:], cross_t, cross_t.t[:])
    for h in range(12):
        S.dma("pool", akT[h, :, 0:PADL], zero_bf.t[:, 0:PADL], reads=[zero_bf])
        S.dma("pool", akT[h, :, PADL + N:NPAD], zero_bf.t[:, 0:PADR], reads=[zero_bf])
    for r0 in list(range(0, PADL, 128)) + list(range(PADL + N, NPAD, 128)):
        S.dma("pool", av[r0:r0 + 128, :], zero_bf.t[:, :], reads=[zero_bf])

    ci = 0
    for l in range(DEPTH):
        for c in range(INW // 128):
            st = wstage[0]
            w16 = wt[ci % 3]
            S.dma("sp", st.t[:], w_in[l, :, c * 128:(c + 1) * 128].rearrange("(kc p) j -> p kc j", p=128),
                  writes=[st])
            cp("dve" if ci % 2 == 0 else "pool", w16, w16.t[:], st, st.t[:])
            S.dma("pool", wbf[l, c].rearrange("p (kc j) -> p kc j", j=128), w16.t[:], reads=[w16])
            ci += 1
        for (src, dst, nk) in ((w_pa, wpabf, 4), (w_pb, wpbbf, 8), (w_o, wobf, 8)):
            for oc in range(8):
                st = wstage[0]
                w16 = wt[ci % 3]
                S.dma("sp", st.t[:, 0:nk, :],
                      src[l, :, oc * 128:(oc + 1) * 128].rearrange("(kc p) j -> p kc j", p=128), writes=[st])
                cp("dve" if ci % 2 == 0 else "pool", w16, w16.t[:, 0:nk, :], st, st.t[:, 0:nk, :])
                S.dma("pool", dst[l].rearrange("p (kc j) -> p kc j", j=D)[:, :, oc * 128:(oc + 1) * 128],
                      w16.t[:, 0:nk, :], reads=[w16])
                ci += 1
    S.barrier()
    import os
    STOP = int(os.environ.get("KSTOP", "99"))

    wcnt = [0]

    def load_w(l, c):
        w16 = wt[wcnt[0] % 3]
        wcnt[0] += 1
        S.dma("sp", w16.t[:], wbf[l, c].rearrange("p (kc j) -> p kc j", j=128), writes=[w16])
        return w16

    def proj_fm(l, c, pst):
        w16 = load_w(l, c)
        for kc in range(8):
            mm(pst, pst.t[:], w16, w16.t[:, kc, :], xb, xb.t[:, kc, :], kc == 0, kc == 7)

    xcnt = [0]

    def load_x(l, t0):
        src = xT if l == 0 else x1T
        for kc in range(8):
            xs = xf[xcnt[0] % 2]
            xcnt[0] += 1
            S.dma("sp", xs.t[:], src[kc * 128:(kc + 1) * 128, t0:t0 + TQ], writes=[xs])
            cp("pool" if kc % 2 else "dve", xb, xb.t[:, kc, :], xs, xs.t[:])

    def load_rope(t0):
        S.dma("sp", cos_t.t[:], cosT[:, t0:t0 + TQ], writes=[cos_t])
        S.dma("sp", sin_t.t[:], sinT[:, t0:t0 + TQ], writes=[sin_t])

    def norm_rope(pst, c, gain_t, out_t, out_ap):
        act(tA, tA.t[:], pst, pst.t[:], AF.Identity, bias=bias_fm.t[:, c:c + 1], extra_reads=[bias_fm])
        tt("dve", tbf, tbf.t[:], tA, tA.t[:], tA, tA.t[:], ALU.mult)
        mm(PS_G, PS_G.t[:], ones_bf, ones_bf.t[:, 0:128], tbf, tbf.t[:], True, True)
        act(tB, tB.t[:], PS_G, PS_G.t[:], AF.Sqrt, bias=eps_rms.t[:, 0:1], scale=1.0 / HD, extra_reads=[eps_rms])
        S.op("dve", lambda e: e.reciprocal(out=tB.t[:], in_=tB.t[:]), reads=[tB], writes=[tB])
        stt("dve", tC, tC.t[:], tA, tA.t[:], gain_t.t[:, 0:1], tB, tB.t[:], ALU.mult, ALU.mult,
            extra_reads=[gain_t])
        cp("dve", tbf2, tbf2.t[:], tC, tC.t[:])
        mm(PS_G, PS_G.t[:], perm_b, perm_b.t[:], tbf2, tbf2.t[:], True, True)
        tt("dve", tD, tD.t[:], PS_G, PS_G.t[:], sin_t, sin_t.t[:], ALU.mult)
        tt("dve", tC, tC.t[:], tC, tC.t[:], cos_t, cos_t.t[:], ALU.mult)
        tt("dve", out_t, out_ap, tC, tC.t[:], tD, tD.t[:], ALU.add)

    vcols = [(OFF_BV // 128 + i, "b", i) for i in range(2)] + [(OFF_AV // 128 + i, "a", i) for i in range(12)]

    for l in range(DEPTH):
        if STOP <= 2 * l:
            break
        for c in range(INW // 128):
            S.dma("sp", bias_fm.t[:, c:c + 1], b_in[l, c * 128:(c + 1) * 128].rearrange("(p o) -> p o", o=1),
                  writes=[bias_fm], slow=True)
        S.dma("sp", z.t[0:1, 0, 0:256], b_in[l:l + 1, OFF_BV:OFF_BV + 256], writes=[z])
        S.dma("sp", z.t[0:1, 1:4, :], b_in[l:l + 1, OFF_AV:OFF_AV + 1536].rearrange("o (a b) -> o a b", b=512),
              writes=[z])
        cp("dve", bias_row, bias_row.t[0:1, 0:256], z, z.t[0:1, 0, 0:256])
        cp("dve", bias_row, bias_row.t[0:1, 256:1792].rearrange("o (a b) -> o a b", b=512), z, z.t[0:1, 1:4, :])
        S.dma("sp", qg_t.t[:], q_gain[l].rearrange("(p o) -> p o", o=1), writes=[qg_t], slow=True)
        S.dma("sp", kg_t.t[:], k_gain[l].rearrange("(p o) -> p o", o=1), writes=[kg_t], slow=True)
        for c in range(8):
            S.dma("sp", lng_t.t[:, c:c + 1], ln_g[l, c * 128:(c + 1) * 128].rearrange("(p o) -> p o", o=1),
                  writes=[lng_t], slow=True)
            S.dma("sp", lnb_t.t[:, c:c + 1], ln_b[l, c * 128:(c + 1) * 128].rearrange("(p o) -> p o", o=1),
                  writes=[lnb_t], slow=True)
        S.dma("sp", wpa_t.t[:], wpabf[l].rearrange("p (kc j) -> p kc j", j=D), writes=[wpa_t])
        S.dma("sp", wpb_t.t[:], wpbbf[l].rearrange("p (kc j) -> p kc j", j=D), writes=[wpb_t])
        S.dma("sp", wo_t.t[:], wobf[l].rearrange("p (kc j) -> p kc j", j=D), writes=[wo_t])

        sc = 0
        for ti in range(NT):
            t0 = ti * TQ
            load_x(l, t0)
            load_rope(t0)
            for kvh in range(2):
                c = OFF_BK // 128 + kvh
                proj_fm(l, c, PS_S[kvh])
                so = stg[sc % 2]
                sc += 1
                norm_rope(PS_S[kvh], c, kg_t, so, so.t[:])
                S.dma("pool", kbT[kvh, :, t0:t0 + TQ], so.t[:], reads=[so])
            for h in range(12):
                c = OFF_AK // 128 + h
                pst = PS_O[h % 4]
                proj_fm(l, c, pst)
                so = stg[sc % 2]
                sc += 1
                act(so, so.t[:], pst, pst.t[:], AF.Identity, bias=bias_fm.t[:, c:c + 1], extra_reads=[bias_fm])
                S.dma("pool", akT[h, :, PADL + t0:PADL + t0 + TQ], so.t[:], reads=[so])
            for vi, (c, which, i) in enumerate(vcols):
                w16 = load_w(l, c)
                pst = PS_O[vi % 4]
                for tc in range(4):
                    for kc in range(8):
                        mm(pst, pst.t[:, tc * 128:(tc + 1) * 128], xb, xb.t[:, kc, tc * 128:(tc + 1) * 128],
                           w16, w16.t[:, kc, :], kc == 0 and tc == 0, False)
                    mm(pst, pst.t[:, tc * 128:(tc + 1) * 128], ones_bf, ones_bf.t[0:1, 0:128],
                       bias_row, bias_row.t[0:1, vi * 128:(vi + 1) * 128], False, True)
                so = stg[sc % 2]
                sc += 1
                if vi % 2 == 0:
                    act(so, so.t[:], pst, pst.t[:], AF.Identity)
                else:
                    cp("dve", so, so.t[:], pst, pst.t[:])
                if which == "b":
                    dst = vb[t0:t0 + TQ, i * 128:(i + 1) * 128]
                else:
                    dst = av[PADL + t0:PADL + t0 + TQ, i * 128:(i + 1) * 128]
                S.dma("pool", dst.rearrange("(tc p) j -> p tc j", p=128),
                      so.t[:].rearrange("p (tc j) -> p tc j", j=128), reads=[so])
        S.barrier()
        if STOP <= 2 * l + 1:
            break

        kvc = 0
        pc = 0
        awc = 0
        for ti in range(NT):
            t0 = ti * TQ
            qhalf = ti // TPH
            load_x(l, t0)
            load_rope(t0)
            for kvh in range(2):
                for a in range(4):
                    h = 4 * kvh + a
                    c = OFF_BQ // 128 + h
                    proj_fm(l, c, PS_S[a % 2])
                    norm_rope(PS_S[a % 2], c, qg_t, qb[a], qb[a].t[:])
                first = True
                ngrp = N // 2048
                for grp in range(ngrp):
                    kg = Kg[kvc % 2]
                    vg = Vg[kvc % 2]
                    kvc += 1
                    k0 = grp * 2048
                    S.dma("sp", kg.t[:], kbT[kvh, :, k0:k0 + 2048], writes=[kg])
                    S.dma("sp", vg.t[:], vb[k0:k0 + 2048, kvh * 128:(kvh + 1) * 128].rearrange(
                        "(m p) j -> p m j", p=128), writes=[vg])
                    khalf = k0 // HALF
                    bias = 0.0
                    is_cross = khalf != qhalf
                    if is_cross:
                        vflat = vg.t[:].rearrange("p m j -> p (m j)")
                        ts("dve", vg, vflat, vg, vflat, cross_t.t[:, 0:1], 0.0, ALU.mult, ALU.add,
                           extra_reads=[cross_t])
                    lcol_t = cross_bf if is_cross else ones_bf
                    last_grp = grp == ngrp - 1
                    for m in range(16):
                        last = last_grp and m == 15
                        for a in range(4):
                            pss = PS_S[pc % 2]
                            p_t = Pt[pc % 3]
                            pc += 1
                            mm(pss, pss.t[:], kg, kg.t[:, m * 128:(m + 1) * 128], qb[a], qb[a].t[:], True, True)
                            act(p_t, p_t.t[:], pss, pss.t[:], AF.Exp, scale=SCALE)
                            mm(PS_O[a], PS_O[a].t[:], vg, vg.t[:, m, :], p_t, p_t.t[:], first, last)
                            lt_, lr_ = (PS_L, 32 * a) if a < 3 else (PS_G, 0)
                            mm(lt_, lt_.t[lr_:lr_ + 1, :], lcol_t, lcol_t.t[:, 0:1], p_t, p_t.t[:],
                               first, last)
                        first = False
                for a in (3, 0, 1, 2):
                    h = 4 * kvh + a
                    lt_, lr_ = (PS_L, 32 * a) if a < 3 else (PS_G, 0)
                    finish_head(S, locals(), PS_O[a], lt_, lr_, yb, yb.t[:, h, :], l, OFF_BG // 128 + h)
            for hh in range(4):
                firstA = True
                for g, (window, dil) in enumerate(A_GROUPS):
                    ah = 4 * g + hh
                    c = OFF_AQ // 128 + ah
                    qt = qa[awc % 2]
                    kw = Kw[awc % 2]
                    vw = Vw[awc % 2]
                    awc += 1
                    proj_fm(l, c, PS_G)
                    nq = TQ // dil
                    if dil == 1:
                        act(qt, qt.t[:], PS_G, PS_G.t[:], AF.Identity, bias=bias_fm.t[:, c:c + 1],
                            extra_reads=[bias_fm])
                    else:
                        act(qt, qt.t[:].rearrange("p (r i) -> p r i", r=dil), PS_G,
                            PS_G.t[:].rearrange("p (i r) -> p r i", r=dil), AF.Identity,
                            bias=bias_fm.t[:, c:c + 1], extra_reads=[bias_fm])
                    wlen = {1: 640, 4: 1024, 16: 2560}[dil]
                    w0 = PADL + t0 - 64 * dil
                    S.dma("sp", kw.t[:, 0:wlen], akT[ah, :, w0:w0 + wlen], writes=[kw])
                    if dil == 1:
                        blocks = [(b * 128, 128, 0, b * 128) for b in range(4)]
                        S.dma("sp", vw.t[:, 0:5, :], av[w0:w0 + 640, ah * 128:(ah + 1) * 128].rearrange(
                            "(m p) j -> p m j", p=128), writes=[vw])
                        vidx = lambda bi, cc: bi + cc
                        nk1 = 128
                    else:
                        blocks = [(r * nq, nq, r, 0) for r in range(dil)]
                        nk1 = 128 if dil == 4 else 32
                        for cc in range(2):
                            nk = 128 if cc == 0 else nk1
                            base = w0 + dil * 128 * cc
                            S.dma("sp", vw.t[0:nk, cc * dil:(cc + 1) * dil, :],
                                  av[base:base + dil * nk, ah * 128:(ah + 1) * 128].rearrange(
                                      "(p r) j -> p r j", r=dil), writes=[vw])
                        vidx = lambda bi, cc, dil=dil: cc * dil + bi
                    var = tile_variant(ti, TPH, NT, dil)
                    for cc in range(2):
                        nk = 128 if cc == 0 else nk1
                        pss = PS_S[pc % 2]
                        p_t = Pt[pc % 3]
                        pm_t = Pm[pc % 2]
                        e_t = Et[pc % 2]
                        pc += 1
                        S.dma("sp", e_t.t[:], etab[var, g, hh, cc], writes=[e_t])
                        for bi, (c0, ncol, r, i0) in enumerate(blocks):
                            off = r + dil * (i0 + 128 * cc)
                            lhs = kw.t[:, off:off + (nk - 1) * dil + 1:dil] if dil > 1 else kw.t[:, off:off + nk]
                            mm(pss, pss.t[0:nk, c0:c0 + ncol], kw, lhs, qt, qt.t[:, c0:c0 + ncol], True, True)
                        act(p_t, p_t.t[0:nk, :], pss, pss.t[0:nk, :], AF.Exp, scale=SCALE)
                        tt("pool", pm_t, pm_t.t[0:nk, :], p_t, p_t.t[0:nk, :], e_t, e_t.t[0:nk, :], ALU.mult)
                        for bi, (c0, ncol, r, i0) in enumerate(blocks):
                            if dil == 1:
                                ocols = slice(c0, c0 + ncol)
                            else:
                                ocols = slice(r, r + (ncol - 1) * dil + 1, dil)
                            lastA = (g == 2 and cc == 1 and bi == len(blocks) - 1)
                            mm(PS_O[0], PS_O[0].t[:, ocols], vw, vw.t[0:nk, vidx(bi, cc), :],
                               pm_t, pm_t.t[0:nk, c0:c0 + ncol], firstA, lastA)
                            mm(PS_L, PS_L.t[0:1, ocols], ones_bf, ones_bf.t[0:nk, 0:1],
                               pm_t, pm_t.t[0:nk, c0:c0 + ncol], firstA, lastA)
                            firstA = False
                finish_head(S, locals(), PS_O[0], PS_L, 0, ya, ya.t[:, hh, :], l, OFF_AG // 128 + hh)
            for oc in range(8):
                pa = PS_O[1]
                pbp = PS_O[2]
                for kc in range(4):
                    mm(pa, pa.t[:], wpa_t, wpa_t.t[:, kc, oc * 128:(oc + 1) * 128], ya, ya.t[:, kc, :],
                       kc == 0, kc == 3)
                for kc in range(8):
                    mm(pbp, pbp.t[:], wpb_t, wpb_t.t[:, kc, oc * 128:(oc + 1) * 128], yb, yb.t[:, kc, :],
                       kc == 0, kc == 7)
                cga = OFF_GA // 128 + oc
                proj_fm(l, cga, PS_S[0])
                act(tA, tA.t[:], PS_S[0], PS_S[0].t[:], AF.Sigmoid, bias=bias_fm.t[:, cga:cga + 1],
                    extra_reads=[bias_fm])
                cgb = OFF_GB // 128 + oc
                proj_fm(l, cgb, PS_S[1])
                act(tB, tB.t[:], PS_S[1], PS_S[1].t[:], AF.Sigmoid, bias=bias_fm.t[:, cgb:cgb + 1],
                    extra_reads=[bias_fm])
                tt("dve", tC, tC.t[:], pa, pa.t[:], tA, tA.t[:], ALU.mult)
                tt("dve", tD, tD.t[:], pbp, pbp.t[:], tB, tB.t[:], ALU.mult)
                tt("dve", mrg, mrg.t[:, oc, :], tC, tC.t[:], tD, tD.t[:], ALU.add)
            src = xT if l == 0 else x1T
            for oc in range(8):
                pso = PS_O[oc % 2 + 1]
                for kc in range(8):
                    mm(pso, pso.t[:], wo_t, wo_t.t[:, kc, oc * 128:(oc + 1) * 128], mrg, mrg.t[:, kc, :],
                       kc == 0, kc == 7)
                xs = xf[xcnt[0] % 2]
                xcnt[0] += 1
                S.dma("sp", xs.t[:], src[oc * 128:(oc + 1) * 128, t0:t0 + TQ], writes=[xs])
                stt("dve", z, z.t[:, oc, :], xs, xs.t[:], ALPHA, pso, pso.t[:], ALU.mult, ALU.add)
                cp("pool", tbf if oc % 2 == 0 else tbf2, (tbf if oc % 2 == 0 else tbf2).t[:], z, z.t[:, oc, :])
                zb = tbf if oc % 2 == 0 else tbf2
                mm(PS_G, PS_G.t[:], ones_bf, ones_bf.t[:, 0:128], zb, zb.t[:], oc == 0, oc == 7)
            ts("dve", tA, tA.t[:], PS_G, PS_G.t[:], 1.0 / D, 0.0, ALU.mult, ALU.add)
            for oc in range(8):
                tt("dve", z, z.t[:, oc, :], z, z.t[:, oc, :], tA, tA.t[:], ALU.subtract)
                zb = tbf if oc % 2 == 0 else tbf2
                tt("pool", zb, zb.t[:], z, z.t[:, oc, :], z, z.t[:, oc, :], ALU.mult)
                mm(PS_S[0], PS_S[0].t[:], ones_bf, ones_bf.t[:, 0:128], zb, zb.t[:], oc == 0, oc == 7)
            act(tB, tB.t[:], PS_S[0], PS_S[0].t[:], AF.Sqrt, bias=eps_ln.t[:, 0:1], scale=1.0 / D, extra_reads=[eps_ln])
            S.op("dve", lambda e: e.reciprocal(out=tB.t[:], in_=tB.t[:]), reads=[tB], writes=[tB])
            dst = x1T if l == 0 else yT
            for oc in range(8):
                yo = yout[oc % 2]
                tt("dve", tC, tC.t[:], z, z.t[:, oc, :], tB, tB.t[:], ALU.mult)
                ts("dve", yo, yo.t[:], tC, tC.t[:], lng_t.t[:, oc:oc + 1], lnb_t.t[:, oc:oc + 1],
                   ALU.mult, ALU.add, extra_reads=[lng_t, lnb_t])
                S.dma("pool", dst[oc * 128:(oc + 1) * 128, t0:t0 + TQ], yo.t[:], reads=[yo])
        S.barrier()

    sem_names = {}
    keys = list(S.cnt.keys()) + list(S.dma_val.keys())
    for k in keys:
        nm = k if isinstance(k, str) else f"d_{k[1]}_{k[2]}"
        sem_names[k] = es.enter_context(nc.semaphore("s_" + nm))
    block = es.enter_context(nc.Block())

    def replay(engname):
        def run(e):
            for waits, emit, key, inc in S.items[engname]:
                for k, v in waits:
                    e.wait_ge(sem_names[k], v)
                if emit is not None:
                    emit(e).then_inc(sem_names[key], inc)
        return run

    block.tensor(replay("pe"))
    block.scalar(replay("act"))
    block.vector(replay("dve"))
    block.gpsimd(replay("pool"))
    block.sync(replay("sp"))
    es.close()
    return nc


def finish_head(S, env, pso, L_t, r, out_t, out_ap, l, cgate):
    PS_L, PS_G = L_t, env["PS_G"]
    lrow, lhi, llo, lbc = env["lrow"], env["lhi"], env["llo"], env["lbc"]
    ones_bf, gsil, bias_fm, tD = env["ones_bf"], env["gsil"], env["bias_fm"], env["tD"]
    act, cp, tt, mm, proj_fm = env["act"], env["cp"], env["tt"], env["mm"], env["proj_fm"]
    S.op("dve", lambda e: e.reciprocal(out=lrow.t[r:r + 1, :], in_=PS_L.t[r:r + 1, :]), reads=[PS_L], writes=[lrow])
    cp("dve", lhi, lhi.t[r:r + 1, :], lrow, lrow.t[r:r + 1, :])
    tt("dve", lrow, lrow.t[r:r + 1, :], lrow, lrow.t[r:r + 1, :], lhi, lhi.t[r:r + 1, :], ALU.subtract)
    cp("dve", llo, llo.t[r:r + 1, :], lrow, lrow.t[r:r + 1, :])
    mm(PS_G, PS_G.t[:], ones_bf, ones_bf.t[r:r + 1, 0:128], lhi, lhi.t[r:r + 1, :], True, False)
    mm(PS_G, PS_G.t[:], ones_bf, ones_bf.t[r:r + 1, 0:128], llo, llo.t[r:r + 1, :], False, True)
    act(lbc, lbc.t[:], PS_G, PS_G.t[:], AF.Identity)
    tt("dve", tD, tD.t[:], pso, pso.t[:], lbc, lbc.t[:], ALU.mult)
    proj_fm(l, cgate, PS_G)
    act(gsil, gsil.t[:], PS_G, PS_G.t[:], AF.Silu, bias=bias_fm.t[:, cgate:cgate + 1], extra_reads=[bias_fm])
    tt("dve", out_t, out_ap, tD, tD.t[:], gsil, gsil.t[:], ALU.mult)


def tile_variant(ti, TPH, NT, dil):
    half = ti // TPH
    k = ti % TPH
    if k == 0:
        v = 1
    elif k == 1:
        v = 2
    elif k == TPH - 2:
        v = 3
    elif k == TPH - 1:
        v = 4
    else:
        return 0
    return v + 4 * half


VAR_TILE = None


def make_etab(HALF, conn):
    N = 2 * HALF
    TPH = HALF // TQ
    NT = N // TQ
    slopes = (2.0 ** (-8.0 * np.arange(1, 13) / 12)).reshape(3, 4)
    rep = {}
    for ti in range(NT):
        v = tile_variant(ti, TPH, NT, 16)
        rep.setdefault(v, ti)
    out = np.zeros((9, 3, 4, 2, 128, TQ), np.float32)
    j = np.arange(128)[:, None]
    for v, ti in rep.items():
        t0 = ti * TQ
        half = ti // TPH
        lo, hi = (0, N) if conn else (half * HALF, (half + 1) * HALF)
        for g, (window, dil) in enumerate(A_GROUPS):
            nq = TQ // dil
            col = np.arange(TQ)[None, :]
            if dil == 1:
                r = np.zeros_like(col)
                i0 = (col // 128) * 128
                i = col % 128
            else:
                r = col // nq
                i0 = np.zeros_like(col)
                i = col % nq
            for cc in range(2):
                kidx = i0 - 64 + 128 * cc + j
                qidx = i0 + i
                rel = kidx - qidx
                ktok = t0 + r + dil * kidx
                valid = (np.abs(rel) <= 64) & (ktok >= lo) & (ktok < hi)
                for hh in range(4):
                    e = np.exp(-slopes[g, hh] * dil * np.abs(rel).astype(np.float64))
                    out[v, g, hh, cc] = np.where(valid, e, 0.0)
    return out.astype(ml_dtypes.bfloat16)


def make_rope(HALF, conn):
    N = 2 * HALF
    t = np.arange(N)
    p = t if conn else t % HALF
    row = (p // GRID_W).astype(np.float32)
    colp = (p % GRID_W).astype(np.float32)
    axis_dim = HD // 2
    inv = (10000.0 ** (-np.arange(0, axis_dim, 2, dtype=np.float32) / axis_dim)).astype(np.float32)
    ar = row[:, None] * inv[None]
    ac = colp[:, None] * inv[None]
    ang = np.concatenate([ar, ar, ac, ac], -1)
    cos = np.cos(ang).astype(np.float32).T
    sin = np.sin(ang).astype(np.float32).T.copy()
    sin[0:32] *= -1.0
    sin[64:96] *= -1.0
    return np.ascontiguousarray(cos), np.ascontiguousarray(sin)


def make_perm():
    P = np.zeros((128, 128), np.float32)
    for m in range(128):
        k = m + 32 if (m % 64) < 32 else m - 32
        P[k, m] = 1.0
    return P


_NC_CACHE = {}


def run_cores(core_x, conns, weights, HALF):
    if HALF not in _NC_CACHE:
        _NC_CACHE[HALF] = build(HALF)
    nc = _NC_CACHE[HALF]
    perm = make_perm()
    consts = {}
    for conn in set(conns):
        cos, sin = make_rope(HALF, conn)
        consts[conn] = dict(cosT=cos, sinT=sin, etab=make_etab(HALF, conn),
                            cross=np.full((128, 1), 1.0 if conn else 0.0, np.float32))
    in_maps = []
    for x, conn in zip(core_x, conns):
        m = dict(weights)
        m["xT"] = np.ascontiguousarray(x.T)
        m["permM"] = perm
        m.update(consts[conn])
        in_maps.append(m)
    res = run_bass_kernel_spmd(nc, in_maps, core_ids=list(range(len(in_maps))))
    return [np.ascontiguousarray(r["yT"].T) for r in res.results]


def kernel(x_prompt, x_sample, w_in, b_in, q_gain, k_gain, w_proj_a, w_proj_b, w_out, ln_g, ln_b):
    f = lambda a: np.ascontiguousarray(np.asarray(a, dtype=np.float32))
    x_prompt, x_sample = f(x_prompt), f(x_sample)
    weights = dict(w_in=f(w_in), b_in=f(b_in), q_gain=f(q_gain), k_gain=f(k_gain), w_proj_a=f(w_proj_a),
                   w_proj_b=f(w_proj_b), w_out=f(w_out), ln_g=f(ln_g), ln_b=f(ln_b))
    HALF = x_prompt.shape[1]
    c0 = x_prompt[0:2].reshape(2 * HALF, D)
    c1 = x_prompt[2:4].reshape(2 * HALF, D)
    c2 = x_sample[0]
    xs = [c0, c1, c2]
    conns = [False, False, True]
    outs = run_cores(xs, conns, weights, HALF)
    y_prompt = np.concatenate([outs[0].reshape(2, HALF, D), outs[1].reshape(2, HALF, D)], 0)
    y_sample = outs[2].reshape(1, 2 * HALF, D)
    return (y_prompt.astype(np.float32), y_sample.astype(np.float32))
```

```python
import contextlib
import math
import numpy as np
import ml_dtypes
import concourse.bass as bass
import concourse.mybir as mybir
from concourse.bass_utils import run_bass_kernel_spmd

F32 = mybir.dt.float32
BF16 = mybir.dt.bfloat16
AF = mybir.ActivationFunctionType
ALU = mybir.AluOpType

D = 1024
HD = 128
INW = 9728
OFF_AQ, OFF_AK, OFF_AV, OFF_AG = 0, 1536, 3072, 4608
OFF_BQ, OFF_BK, OFF_BV, OFF_BG = 5120, 6144, 6400, 6656
OFF_GA, OFF_GB = 7680, 8704
DEPTH = 2
GRID_W = 64
A_GROUPS = ((128, 1), (512, 4), (2048, 16))
ALPHA = (2 * DEPTH) ** 0.25
RMS_EPS = 1e-6
LN_EPS = 1e-5
SCALE = 1.0 / math.sqrt(HD)
TQ = 512
PADL = 1024
PADR = 1024
NEG = -700.0
NDMA = 8


class Tl:
    __slots__ = ("t", "lw", "rd")

    def __init__(self, t):
        self.t = t
        self.lw = None
        self.rd = {}


class Sched:
    ENGS = ("pe", "act", "dve", "pool", "sp")

    def __init__(self):
        self.items = {e: [] for e in self.ENGS}
        self.cnt = {e: 0 for e in ("pe", "act", "dve", "pool")}
        self.dma_n = {"sp": 0, "pool": 0, "act": 0}
        self.dma_val = {}
        self.seen = {e: {} for e in self.ENGS}

    def _deps(self, eng, reads, writes):
        deps = {}

        def add(k, v):
            if deps.get(k, 0) < v:
                deps[k] = v
        for t in reads:
            if t.lw is not None:
                add(*t.lw)
        for t in writes:
            if t.lw is not None:
                add(*t.lw)
            for k, v in t.rd.items():
                add(k, v)
        waits = []
        for k, v in deps.items():
            if k == "pe" and eng == "pe":
                continue
            if self.seen[eng].get(k, 0) >= v:
                continue
            self.seen[eng][k] = v
            waits.append((k, v))
        return waits

    def _mark(self, me, reads, writes):
        k, v = me
        for t in reads:
            if t.rd.get(k, 0) < v:
                t.rd[k] = v
        for t in writes:
            t.lw = me
            t.rd = {}

    def op(self, eng, emit, reads=(), writes=()):
        waits = self._deps(eng, reads, writes)
        self.cnt[eng] += 1
        me = (eng, self.cnt[eng])
        self._mark(me, reads, writes)
        self.items[eng].append((waits, emit, eng, 1))

    def dma(self, q, out, in_, reads=(), writes=(), slow=False):
        n = self.dma_n[q]
        self.dma_n[q] = n + 1
        key = ("dma", q, n % NDMA)
        prev = self.dma_val.get(key, 0)
        waits = self._deps(q, reads, writes)
        if prev and self.seen[q].get(key, 0) < prev:
            self.seen[q][key] = prev
            waits.append((key, prev))
        val = prev + 16
        self.dma_val[key] = val
        self._mark((key, val), reads, writes)
        if slow:
            emit = lambda e: e.dma_start(out=out, in_=in_, allow_slow_non_contiguous=True)
        else:
            emit = lambda e: e.dma_start(out=out, in_=in_)
        self.items[q].append((waits, emit, key, 16))

    def barrier(self):
        allv = dict(self.cnt)
        allv.update(self.dma_val)
        for e in self.ENGS:
            waits = []
            for k, v in allv.items():
                if v and k != e and self.seen[e].get(k, 0) < v:
                    self.seen[e][k] = v
                    waits.append((k, v))
            if waits:
                self.items[e].append((waits, None, None, 0))


def build(HALF):
    N = 2 * HALF
    NT = N // TQ
    TPH = HALF // TQ
    nc = bass.Bass("TRN2", target_bir_lowering=False)
    S = Sched()
    es = contextlib.ExitStack()

    def dram(name, shape, dt, kind="Internal"):
        return nc.dram_tensor(name, list(shape), dt, kind=kind)

    xT = dram("xT", [D, N], F32, "ExternalInput")
    w_in = dram("w_in", [DEPTH, D, INW], F32, "ExternalInput")
    b_in = dram("b_in", [DEPTH, INW], F32, "ExternalInput")
    q_gain = dram("q_gain", [DEPTH, HD], F32, "ExternalInput")
    k_gain = dram("k_gain", [DEPTH, HD], F32, "ExternalInput")
    w_pa = dram("w_proj_a", [DEPTH, 512, D], F32, "ExternalInput")
    w_pb = dram("w_proj_b", [DEPTH, D, D], F32, "ExternalInput")
    w_o = dram("w_out", [DEPTH, D, D], F32, "ExternalInput")
    ln_g = dram("ln_g", [DEPTH, D], F32, "ExternalInput")
    ln_b = dram("ln_b", [DEPTH, D], F32, "ExternalInput")
    cosT = dram("cosT", [HD, N], F32, "ExternalInput")
    sinT = dram("sinT", [HD, N], F32, "ExternalInput")
    permM = dram("permM", [HD, HD], F32, "ExternalInput")
    cross = dram("cross", [128, 1], F32, "ExternalInput")
    NVAR = 9
    etab = dram("etab", [NVAR, 3, 4, 2, 128, TQ], BF16, "ExternalInput")
    yT = dram("yT", [D, N], F32, "ExternalOutput")
    wbf = dram("wbf", [DEPTH, INW // 128, 128, 8 * 128], BF16)
    wpabf = dram("wpabf", [DEPTH, 128, 4 * D], BF16)
    wpbbf = dram("wpbbf", [DEPTH, 128, 8 * D], BF16)
    wobf = dram("wobf", [DEPTH, 128, 8 * D], BF16)
    x1T = dram("x1T", [D, N], F32)
    NPAD = PADL + N + PADR
    akT = dram("akT", [12, 128, NPAD], BF16)
    av = dram("av", [NPAD, 1536], BF16)
    kbT = dram("kbT", [2, 128, N], BF16)
    vb = dram("vb", [N, 256], BF16)

    def sb(name, shape, dt):
        return Tl(es.enter_context(nc.sbuf_tensor(name, list(shape), dt)))

    def psum(name):
        return Tl(es.enter_context(nc.psum_tensor(name, [128, TQ], F32)))

    ones_bf = sb("ones_bf", [128, 512], BF16)
    zero_bf = sb("zero_bf", [128, 1536], BF16)
    perm_f = sb("perm_f", [128, 128], F32)
    perm_b = sb("perm_b", [128, 128], BF16)
    cross_t = sb("cross_t", [128, 1], F32)
    cross_bf = sb("cross_bf", [128, 1], BF16)
    eps_rms = sb("eps_rms", [128, 1], F32)
    eps_ln = sb("eps_ln", [128, 1], F32)
    bias_fm = sb("bias_fm", [128, INW // 128], F32)
    bias_row = sb("bias_row", [1, 1792], BF16)
    qg_t = sb("qg_t", [128, 1], F32)
    kg_t = sb("kg_t", [128, 1], F32)
    lng_t = sb("lng_t", [128, 8], F32)
    lnb_t = sb("lnb_t", [128, 8], F32)
    xf = [sb(f"xf{i}", [128, TQ], F32) for i in range(2)]
    xb = sb("xb", [128, 8, TQ], BF16)
    wstage = [sb(f"wstage{i}", [128, 8, 128], F32) for i in range(1)]
    wt = [sb(f"wt{i}", [128, 8, 128], BF16) for i in range(3)]
    wpa_t = sb("wpa_t", [128, 4, D], BF16)
    wpb_t = sb("wpb_t", [128, 8, D], BF16)
    wo_t = sb("wo_t", [128, 8, D], BF16)
    cos_t = sb("cos_t", [128, TQ], F32)
    sin_t = sb("sin_t", [128, TQ], F32)
    tA = sb("tA", [128, TQ], F32)
    tB = sb("tB", [128, TQ], F32)
    tC = sb("tC", [128, TQ], F32)
    tD = sb("tD", [128, TQ], F32)
    tbf = sb("tbf", [128, TQ], BF16)
    tbf2 = sb("tbf2", [128, TQ], BF16)
    stg = [sb(f"stg{i}", [128, TQ], BF16) for i in range(2)]
    qb = [sb(f"qb{i}", [128, TQ], BF16) for i in range(4)]
    gsil = sb("gsil", [128, TQ], F32)
    yb = sb("yb", [128, 8, TQ], BF16)
    ya = sb("ya", [128, 4, TQ], BF16)
    Pt = [sb(f"Pt{i}", [128, TQ], BF16) for i in range(3)]
    Pm = [sb(f"Pm{i}", [128, TQ], BF16) for i in range(2)]
    Et = [sb(f"Et{i}", [128, TQ], BF16) for i in range(2)]
    Kg = [sb(f"Kg{i}", [128, 2048], BF16) for i in range(2)]
    Vg = [sb(f"Vg{i}", [128, 16, 128], BF16) for i in range(2)]
    Kw = [sb(f"Kw{i}", [128, 2560], BF16) for i in range(2)]
    Vw = [sb(f"Vw{i}", [128, 32, 128], BF16) for i in range(2)]
    qa = [sb(f"qa{i}", [128, TQ], BF16) for i in range(2)]
    lrow = sb("lrow", [128, TQ], F32)
    lhi = sb("lhi", [128, TQ], BF16)
    llo = sb("llo", [128, TQ], BF16)
    lbc = sb("lbc", [128, TQ], F32)
    mrg = sb("mrg", [128, 8, TQ], BF16)
    z = sb("z", [128, 8, TQ], F32)
    yout = [sb(f"yout{i}", [128, TQ], F32) for i in range(2)]
    ps = [psum(f"ps{i}") for i in range(8)]
    PS_S = [ps[0], ps[1]]
    PS_O = [ps[2], ps[3], ps[4], ps[5]]
    PS_L = ps[6]
    PS_G = ps[7]

    def mm(out_t, out_ap, lhs_t, lhs_ap, rhs_t, rhs_ap, start, stop):
        S.op("pe", lambda e: e.matmul(out_ap, lhsT=lhs_ap, rhs=rhs_ap, start=start, stop=stop,
                                      skip_group_check=True),
             reads=[lhs_t, rhs_t] + ([] if start else [out_t]), writes=[out_t])

    def act(out_t, out_ap, in_t, in_ap, func, bias=0.0, scale=1.0, extra_reads=()):
        S.op("act", lambda e: e.activation(out=out_ap, in_=in_ap, func=func, bias=bias, scale=scale),
             reads=[in_t] + list(extra_reads), writes=[out_t])

    def tt(eng, out_t, out_ap, a_t, a_ap, b_t, b_ap, op):
        S.op(eng, lambda e: e.tensor_tensor(out=out_ap, in0=a_ap, in1=b_ap, op=op),
             reads=[a_t, b_t], writes=[out_t])

    def ts(eng, out_t, out_ap, a_t, a_ap, s1, s2, op0, op1, extra_reads=()):
        if s2 is None:
            S.op(eng, lambda e: e.tensor_scalar(out=out_ap, in0=a_ap, scalar1=s1, scalar2=None, op0=op0),
                 reads=[a_t] + list(extra_reads), writes=[out_t])
        else:
            S.op(eng, lambda e: e.tensor_scalar(out=out_ap, in0=a_ap, scalar1=s1, scalar2=s2, op0=op0, op1=op1),
                 reads=[a_t] + list(extra_reads), writes=[out_t])

    def stt(eng, out_t, out_ap, a_t, a_ap, sc, b_t, b_ap, op0, op1, extra_reads=()):
        S.op(eng, lambda e: e.scalar_tensor_tensor(out=out_ap, in0=a_ap, scalar=sc, in1=b_ap, op0=op0, op1=op1),
             reads=[a_t, b_t] + list(extra_reads), writes=[out_t])

    def cp(eng, out_t, out_ap, in_t, in_ap):
        S.op(eng, lambda e: e.tensor_copy(out=out_ap, in_=in_ap), reads=[in_t], writes=[out_t])

    def mset(eng, t, ap, val):
        S.op(eng, lambda e: e.memset(ap, val), writes=[t])

    mset("dve", ones_bf, ones_bf.t[:], 1.0)
    mset("dve", zero_bf, zero_bf.t[:], 0.0)
    mset("dve", eps_rms, eps_rms.t[:], RMS_EPS)
    mset("dve", eps_ln, eps_ln.t[:], LN_EPS)
    S.dma("sp", perm_f.t[:], permM[:, :], writes=[perm_f])
    cp("dve", perm_b, perm_b.t[:], perm_f, perm_f.t[:])
    S.dma("sp", cross_t.t[:], cross[:, :], writes=[cross_t])
    cp("dve", cross_bf, cross_bf.t[:], cross_t, cross_t.t[:])
    for h in range(12):
        S.dma("pool", akT[h, :, 0:PADL], zero_bf.t[:, 0:PADL], reads=[zero_bf])
        S.dma("pool", akT[h, :, PADL + N:NPAD], zero_bf.t[:, 0:PADR], reads=[zero_bf])
    for r0 in list(range(0, PADL, 128)) + list(range(PADL + N, NPAD, 128)):
        S.dma("pool", av[r0:r0 + 128, :], zero_bf.t[:, :], reads=[zero_bf])

    ci = 0
    for l in range(DEPTH):
        for c in range(INW // 128):
            st = wstage[0]
            w16 = wt[ci % 3]
            S.dma("sp", st.t[:], w_in[l, :, c * 128:(c + 1) * 128].rearrange("(kc p) j -> p kc j", p=128),
                  writes=[st])
            cp("dve" if ci % 2 == 0 else "pool", w16, w16.t[:], st, st.t[:])
            S.dma("pool", wbf[l, c].rearrange("p (kc j) -> p kc j", j=128), w16.t[:], reads=[w16])
            ci += 1
        for (src, dst, nk) in ((w_pa, wpabf, 4), (w_pb, wpbbf, 8), (w_o, wobf, 8)):
            for oc in range(8):
                st = wstage[0]
                w16 = wt[ci % 3]
                S.dma("sp", st.t[:, 0:nk, :],
                      src[l, :, oc * 128:(oc + 1) * 128].rearrange("(kc p) j -> p kc j", p=128), writes=[st])
                cp("dve" if ci % 2 == 0 else "pool", w16, w16.t[:, 0:nk, :], st, st.t[:, 0:nk, :])
                S.dma("pool", dst[l].rearrange("p (kc j) -> p kc j", j=D)[:, :, oc * 128:(oc + 1) * 128],
                      w16.t[:, 0:nk, :], reads=[w16])
                ci += 1
    S.barrier()
    import os
    STOP = int(os.environ.get("KSTOP", "99"))

    wcnt = [0]

    def load_w(l, c):
        w16 = wt[wcnt[0] % 3]
        wcnt[0] += 1
        S.dma("sp", w16.t[:], wbf[l, c].rearrange("p (kc j) -> p kc j", j=128), writes=[w16])
        return w16

    def proj_fm(l, c, pst):
        w16 = load_w(l, c)
        for kc in range(8):
            mm(pst, pst.t[:], w16, w16.t[:, kc, :], xb, xb.t[:, kc, :], kc == 0, kc == 7)

    xcnt = [0]

    def load_x(l, t0):
        src = xT if l == 0 else x1T
        for kc in range(8):
            xs = xf[xcnt[0] % 2]
            xcnt[0] += 1
            S.dma("sp", xs.t[:], src[kc * 128:(kc + 1) * 128, t0:t0 + TQ], writes=[xs])
            cp("pool" if kc % 2 else "dve", xb, xb.t[:, kc, :], xs, xs.t[:])

    def load_rope(t0):
        S.dma("sp", cos_t.t[:], cosT[:, t0:t0 + TQ], writes=[cos_t])
        S.dma("sp", sin_t.t[:], sinT[:, t0:t0 + TQ], writes=[sin_t])

    def norm_rope(pst, c, gain_t, out_t, out_ap):
        act(tA, tA.t[:], pst, pst.t[:], AF.Identity, bias=bias_fm.t[:, c:c + 1], extra_reads=[bias_fm])
        tt("dve", tbf, tbf.t[:], tA, tA.t[:], tA, tA.t[:], ALU.mult)
        mm(PS_G, PS_G.t[:], ones_bf, ones_bf.t[:, 0:128], tbf, tbf.t[:], True, True)
        act(tB, tB.t[:], PS_G, PS_G.t[:], AF.Sqrt, bias=eps_rms.t[:, 0:1], scale=1.0 / HD, extra_reads=[eps_rms])
        S.op("dve", lambda e: e.reciprocal(out=tB.t[:], in_=tB.t[:]), reads=[tB], writes=[tB])
        stt("dve", tC, tC.t[:], tA, tA.t[:], gain_t.t[:, 0:1], tB, tB.t[:], ALU.mult, ALU.mult,
            extra_reads=[gain_t])
        cp("dve", tbf2, tbf2.t[:], tC, tC.t[:])
        mm(PS_G, PS_G.t[:], perm_b, perm_b.t[:], tbf2, tbf2.t[:], True, True)
        tt("dve", tD, tD.t[:], PS_G, PS_G.t[:], sin_t, sin_t.t[:], ALU.mult)
        tt("dve", tC, tC.t[:], tC, tC.t[:], cos_t, cos_t.t[:], ALU.mult)
        tt("dve", out_t, out_ap, tC, tC.t[:], tD, tD.t[:], ALU.add)

    vcols = [(OFF_BV // 128 + i, "b", i) for i in range(2)] + [(OFF_AV // 128 + i, "a", i) for i in range(12)]

    for l in range(DEPTH):
        if STOP <= 2 * l:
            break
        for c in range(INW // 128):
            S.dma("sp", bias_fm.t[:, c:c + 1], b_in[l, c * 128:(c + 1) * 128].rearrange("(p o) -> p o", o=1),
                  writes=[bias_fm], slow=True)
        S.dma("sp", z.t[0:1, 0, 0:256], b_in[l:l + 1, OFF_BV:OFF_BV + 256], writes=[z])
        S.dma("sp", z.t[0:1, 1:4, :], b_in[l:l + 1, OFF_AV:OFF_AV + 1536].rearrange("o (a b) -> o a b", b=512),
              writes=[z])
        cp("dve", bias_row, bias_row.t[0:1, 0:256], z, z.t[0:1, 0, 0:256])
        cp("dve", bias_row, bias_row.t[0:1, 256:1792].rearrange("o (a b) -> o a b", b=512), z, z.t[0:1, 1:4, :])
        S.dma("sp", qg_t.t[:], q_gain[l].rearrange("(p o) -> p o", o=1), writes=[qg_t], slow=True)
        S.dma("sp", kg_t.t[:], k_gain[l].rearrange("(p o) -> p o", o=1), writes=[kg_t], slow=True)
        for c in range(8):
            S.dma("sp", lng_t.t[:, c:c + 1], ln_g[l, c * 128:(c + 1) * 128].rearrange("(p o) -> p o", o=1),
                  writes=[lng_t], slow=True)
            S.dma("sp", lnb_t.t[:, c:c + 1], ln_b[l, c * 128:(c + 1) * 128].rearrange("(p o) -> p o", o=1),
                  writes=[lnb_t], slow=True)
        S.dma("sp", wpa_t.t[:], wpabf[l].rearrange("p (kc j) -> p kc j", j=D), writes=[wpa_t])
        S.dma("sp", wpb_t.t[:], wpbbf[l].rearrange("p (kc j) -> p kc j", j=D), writes=[wpb_t])
        S.dma("sp", wo_t.t[:], wobf[l].rearrange("p (kc j) -> p kc j", j=D), writes=[wo_t])

        sc = 0
        for ti in range(NT):
            t0 = ti * TQ
            load_x(l, t0)
            load_rope(t0)
            for kvh in range(2):
                c = OFF_BK // 128 + kvh
                proj_fm(l, c, PS_S[kvh])
                so = stg[sc % 2]
                sc += 1
                norm_rope(PS_S[kvh], c, kg_t, so, so.t[:])
                S.dma("pool", kbT[kvh, :, t0:t0 + TQ], so.t[:], reads=[so])
            for h in range(12):
                c = OFF_AK // 128 + h
                pst = PS_O[h % 4]
                proj_fm(l, c, pst)
                so = stg[sc % 2]
                sc += 1
                act(so, so.t[:], pst, pst.t[:], AF.Identity, bias=bias_fm.t[:, c:c + 1], extra_reads=[bias_fm])
                S.dma("pool", akT[h, :, PADL + t0:PADL + t0 + TQ], so.t[:], reads=[so])
            for vi, (c, which, i) in enumerate(vcols):
                w16 = load_w(l, c)
                pst = PS_O[vi % 4]
                for tc in range(4):
                    for kc in range(8):
                        mm(pst, pst.t[:, tc * 128:(tc + 1) * 128], xb, xb.t[:, kc, tc * 128:(tc + 1) * 128],
                           w16, w16.t[:, kc, :], kc == 0 and tc == 0, False)
                    mm(pst, pst.t[:, tc * 128:(tc + 1) * 128], ones_bf, ones_bf.t[0:1, 0:128],
                       bias_row, bias_row.t[0:1, vi * 128:(vi + 1) * 128], False, True)
                so = stg[sc % 2]
                sc += 1
                if vi % 2 == 0:
                    act(so, so.t[:], pst, pst.t[:], AF.Identity)
                else:
                    cp("dve", so, so.t[:], pst, pst.t[:])
                if which == "b":
                    dst = vb[t0:t0 + TQ, i * 128:(i + 1) * 128]
                else:
                    dst = av[PADL + t0:PADL + t0 + TQ, i * 128:(i + 1) * 128]
                S.dma("pool", dst.rearrange("(tc p) j -> p tc j", p=128),
                      so.t[:].rearrange("p (tc j) -> p tc j", j=128), reads=[so])
        S.barrier()
        if STOP <= 2 * l + 1:
            break

        kvc = 0
        pc = 0
        awc = 0
        for ti in range(NT):
            t0 = ti * TQ
            qhalf = ti // TPH
            load_x(l, t0)
            load_rope(t0)
            for kvh in range(2):
                for a in range(4):
                    h = 4 * kvh + a
                    c = OFF_BQ // 128 + h
                    proj_fm(l, c, PS_S[a % 2])
                    norm_rope(PS_S[a % 2], c, qg_t, qb[a], qb[a].t[:])
                GK = min(2048, HALF)
                ngrp = N // GK
                cpg = GK // 128
                grp_buf = {}

                def load_grp(grp):
                    nonlocal kvc
                    kg = Kg[kvc % 2]
                    vg = Vg[kvc % 2]
                    kvc += 1
                    k0 = grp * GK
                    S.dma("sp", kg.t[:, 0:GK], kbT[kvh, :, k0:k0 + GK], writes=[kg])
                    S.dma("sp", vg.t[:, 0:cpg, :], vb[k0:k0 + GK, kvh * 128:(kvh + 1) * 128].rearrange(
                        "(m p) j -> p m j", p=128), writes=[vg])
                    is_cross = (k0 // HALF) != qhalf
                    if is_cross:
                        vflat = vg.t[:, 0:cpg, :].rearrange("p m j -> p (m j)")
                        ts("dve", vg, vflat, vg, vflat, cross_t.t[:, 0:1], 0.0, ALU.mult, ALU.add,
                           extra_reads=[cross_t])
                    grp_buf[grp] = (kg, vg, cross_bf if is_cross else ones_bf)

                units = [(grp, m, a) for grp in range(ngrp) for m in range(cpg) for a in range(4)]
                NU = len(units)
                ubuf = {}

                def issue_qk(u):
                    nonlocal pc
                    grp, m, a = units[u]
                    kg = grp_buf[grp][0]
                    pss = PS_S[pc % 2]
                    p_t = Pt[pc % 3]
                    pc += 1
                    ubuf[u] = p_t
                    mm(pss, pss.t[:], kg, kg.t[:, m * 128:(m + 1) * 128], qb[a], qb[a].t[:], True, True)
                    act(p_t, p_t.t[:], pss, pss.t[:], AF.Exp, scale=SCALE)

                def issue_pv(u):
                    grp, m, a = units[u]
                    kg, vg, lcol_t = grp_buf[grp]
                    p_t = ubuf.pop(u)
                    first = (grp == 0 and m == 0)
                    last = (grp == ngrp - 1 and m == cpg - 1)
                    mm(PS_O[a], PS_O[a].t[:], vg, vg.t[:, m, :], p_t, p_t.t[:], first, last)
                    lt_, lr_ = (PS_L, 32 * a) if a < 3 else (PS_G, 0)
                    mm(lt_, lt_.t[lr_:lr_ + 1, :], lcol_t, lcol_t.t[:, 0:1], p_t, p_t.t[:], first, last)

                load_grp(0)
                if ngrp > 1:
                    load_grp(1)
                issue_qk(0)
                issue_qk(1)
                for u in range(NU):
                    issue_pv(u)
                    grp, m, a = units[u]
                    if m == cpg - 1 and a == 3 and grp + 2 < ngrp:
                        load_grp(grp + 2)
                    if u + 2 < NU:
                        issue_qk(u + 2)
                for a in (3, 0, 1, 2):
                    h = 4 * kvh + a
                    lt_, lr_ = (PS_L, 32 * a) if a < 3 else (PS_G, 0)
                    finish_head(S, locals(), PS_O[a], lt_, lr_, yb, yb.t[:, h, :], l, OFF_BG // 128 + h)
            for hh in range(4):
                firstA = True
                for g, (window, dil) in enumerate(A_GROUPS):
                    ah = 4 * g + hh
                    c = OFF_AQ // 128 + ah
                    qt = qa[awc % 2]
                    kw = Kw[awc % 2]
                    vw = Vw[awc % 2]
                    awc += 1
                    proj_fm(l, c, PS_G)
                    nq = TQ // dil
                    if dil == 1:
                        act(qt, qt.t[:], PS_G, PS_G.t[:], AF.Identity, bias=bias_fm.t[:, c:c + 1],
                            extra_reads=[bias_fm])
                    else:
                        act(qt, qt.t[:].rearrange("p (r i) -> p r i", r=dil), PS_G,
                            PS_G.t[:].rearrange("p (i r) -> p r i", r=dil), AF.Identity,
                            bias=bias_fm.t[:, c:c + 1], extra_reads=[bias_fm])
                    wlen = {1: 640, 4: 1024, 16: 2560}[dil]
                    w0 = PADL + t0 - 64 * dil
                    S.dma("sp", kw.t[:, 0:wlen], akT[ah, :, w0:w0 + wlen], writes=[kw])
                    if dil == 1:
                        blocks = [(b * 128, 128, 0, b * 128) for b in range(4)]
                        S.dma("sp", vw.t[:, 0:5, :], av[w0:w0 + 640, ah * 128:(ah + 1) * 128].rearrange(
                            "(m p) j -> p m j", p=128), writes=[vw])
                        vidx = lambda bi, cc: bi + cc
                        nk1 = 128
                    else:
                        blocks = [(r * nq, nq, r, 0) for r in range(dil)]
                        nk1 = 128 if dil == 4 else 32
                        for cc in range(2):
                            nk = 128 if cc == 0 else nk1
                            base = w0 + dil * 128 * cc
                            S.dma("sp", vw.t[0:nk, cc * dil:(cc + 1) * dil, :],
                                  av[base:base + dil * nk, ah * 128:(ah + 1) * 128].rearrange(
                                      "(p r) j -> p r j", r=dil), writes=[vw])
                        vidx = lambda bi, cc, dil=dil: cc * dil + bi
                    var = tile_variant(ti, TPH, NT, dil)
                    for cc in range(2):
                        nk = 128 if cc == 0 else nk1
                        pss = PS_S[pc % 2]
                        p_t = Pt[pc % 3]
                        pm_t = Pm[pc % 2]
                        e_t = Et[pc % 2]
                        pc += 1
                        S.dma("sp", e_t.t[:], etab[var, g, hh, cc], writes=[e_t])
                        for bi, (c0, ncol, r, i0) in enumerate(blocks):
                            off = r + dil * (i0 + 128 * cc)
                            lhs = kw.t[:, off:off + (nk - 1) * dil + 1:dil] if dil > 1 else kw.t[:, off:off + nk]
                            mm(pss, pss.t[0:nk, c0:c0 + ncol], kw, lhs, qt, qt.t[:, c0:c0 + ncol], True, True)
                        act(p_t, p_t.t[0:nk, :], pss, pss.t[0:nk, :], AF.Exp, scale=SCALE)
                        tt("pool", pm_t, pm_t.t[0:nk, :], p_t, p_t.t[0:nk, :], e_t, e_t.t[0:nk, :], ALU.mult)
                        for bi, (c0, ncol, r, i0) in enumerate(blocks):
                            if dil == 1:
                                ocols = slice(c0, c0 + ncol)
                            else:
                                ocols = slice(r, r + (ncol - 1) * dil + 1, dil)
                            lastA = (g == 2 and cc == 1 and bi == len(blocks) - 1)
                            mm(PS_O[0], PS_O[0].t[:, ocols], vw, vw.t[0:nk, vidx(bi, cc), :],
                               pm_t, pm_t.t[0:nk, c0:c0 + ncol], firstA, lastA)
                            mm(PS_L, PS_L.t[0:1, ocols], ones_bf, ones_bf.t[0:nk, 0:1],
                               pm_t, pm_t.t[0:nk, c0:c0 + ncol], firstA, lastA)
                            firstA = False
                finish_head(S, locals(), PS_O[0], PS_L, 0, ya, ya.t[:, hh, :], l, OFF_AG // 128 + hh)
            for oc in range(8):
                pa = PS_O[1]
                pbp = PS_O[2]
                for kc in range(4):
                    mm(pa, pa.t[:], wpa_t, wpa_t.t[:, kc, oc * 128:(oc + 1) * 128], ya, ya.t[:, kc, :],
                       kc == 0, kc == 3)
                for kc in range(8):
                    mm(pbp, pbp.t[:], wpb_t, wpb_t.t[:, kc, oc * 128:(oc + 1) * 128], yb, yb.t[:, kc, :],
                       kc == 0, kc == 7)
                cga = OFF_GA // 128 + oc
                proj_fm(l, cga, PS_S[0])
                act(tA, tA.t[:], PS_S[0], PS_S[0].t[:], AF.Sigmoid, bias=bias_fm.t[:, cga:cga + 1],
                    extra_reads=[bias_fm])
                cgb = OFF_GB // 128 + oc
                proj_fm(l, cgb, PS_S[1])
                act(tB, tB.t[:], PS_S[1], PS_S[1].t[:], AF.Sigmoid, bias=bias_fm.t[:, cgb:cgb + 1],
                    extra_reads=[bias_fm])
                tt("dve", tC, tC.t[:], pa, pa.t[:], tA, tA.t[:], ALU.mult)
                tt("dve", tD, tD.t[:], pbp, pbp.t[:], tB, tB.t[:], ALU.mult)
                tt("dve", mrg, mrg.t[:, oc, :], tC, tC.t[:], tD, tD.t[:], ALU.add)
            src = xT if l == 0 else x1T
            for oc in range(8):
                pso = PS_O[oc % 2 + 1]
                for kc in range(8):
                    mm(pso, pso.t[:], wo_t, wo_t.t[:, kc, oc * 128:(oc + 1) * 128], mrg, mrg.t[:, kc, :],
                       kc == 0, kc == 7)
                xs = xf[xcnt[0] % 2]
                xcnt[0] += 1
                S.dma("sp", xs.t[:], src[oc * 128:(oc + 1) * 128, t0:t0 + TQ], writes=[xs])
                stt("dve", z, z.t[:, oc, :], xs, xs.t[:], ALPHA, pso, pso.t[:], ALU.mult, ALU.add)
                cp("pool", tbf if oc % 2 == 0 else tbf2, (tbf if oc % 2 == 0 else tbf2).t[:], z, z.t[:, oc, :])
                zb = tbf if oc % 2 == 0 else tbf2
                mm(PS_G, PS_G.t[:], ones_bf, ones_bf.t[:, 0:128], zb, zb.t[:], oc == 0, oc == 7)
            ts("dve", tA, tA.t[:], PS_G, PS_G.t[:], 1.0 / D, 0.0, ALU.mult, ALU.add)
            for oc in range(8):
                tt("dve", z, z.t[:, oc, :], z, z.t[:, oc, :], tA, tA.t[:], ALU.subtract)
                zb = tbf if oc % 2 == 0 else tbf2
                tt("pool", zb, zb.t[:], z, z.t[:, oc, :], z, z.t[:, oc, :], ALU.mult)
                mm(PS_S[0], PS_S[0].t[:], ones_bf, ones_bf.t[:, 0:128], zb, zb.t[:], oc == 0, oc == 7)
            act(tB, tB.t[:], PS_S[0], PS_S[0].t[:], AF.Sqrt, bias=eps_ln.t[:, 0:1], scale=1.0 / D, extra_reads=[eps_ln])
            S.op("dve", lambda e: e.reciprocal(out=tB.t[:], in_=tB.t[:]), reads=[tB], writes=[tB])
            dst = x1T if l == 0 else yT
            for oc in range(8):
                yo = yout[oc % 2]
                tt("dve", tC, tC.t[:], z, z.t[:, oc, :], tB, tB.t[:], ALU.mult)
                ts("dve", yo, yo.t[:], tC, tC.t[:], lng_t.t[:, oc:oc + 1], lnb_t.t[:, oc:oc + 1],
                   ALU.mult, ALU.add, extra_reads=[lng_t, lnb_t])
                S.dma("pool", dst[oc * 128:(oc + 1) * 128, t0:t0 + TQ], yo.t[:], reads=[yo])
        S.barrier()

    sem_names = {}
    keys = list(S.cnt.keys()) + list(S.dma_val.keys())
    for k in keys:
        nm = k if isinstance(k, str) else f"d_{k[1]}_{k[2]}"
        sem_names[k] = es.enter_context(nc.semaphore("s_" + nm))
    block = es.enter_context(nc.Block())

    def replay(engname):
        def run(e):
            for waits, emit, key, inc in S.items[engname]:
                for k, v in waits:
                    e.wait_ge(sem_names[k], v)
                if emit is not None:
                    emit(e).then_inc(sem_names[key], inc)
        return run

    block.tensor(replay("pe"))
    block.scalar(replay("act"))
    block.vector(replay("dve"))
    block.gpsimd(replay("pool"))
    block.sync(replay("sp"))
    es.close()
    return nc


def finish_head(S, env, pso, L_t, r, out_t, out_ap, l, cgate):
    PS_L, PS_G = L_t, env["PS_G"]
    lrow, lhi, llo, lbc = env["lrow"], env["lhi"], env["llo"], env["lbc"]
    ones_bf, gsil, bias_fm, tD = env["ones_bf"], env["gsil"], env["bias_fm"], env["tD"]
    act, cp, tt, mm, proj_fm = env["act"], env["cp"], env["tt"], env["mm"], env["proj_fm"]
    S.op("dve", lambda e: e.reciprocal(out=lrow.t[r:r + 1, :], in_=PS_L.t[r:r + 1, :]), reads=[PS_L], writes=[lrow])
    cp("dve", lhi, lhi.t[r:r + 1, :], lrow, lrow.t[r:r + 1, :])
    tt("dve", lrow, lrow.t[r:r + 1, :], lrow, lrow.t[r:r + 1, :], lhi, lhi.t[r:r + 1, :], ALU.subtract)
    cp("dve", llo, llo.t[r:r + 1, :], lrow, lrow.t[r:r + 1, :])
    mm(PS_G, PS_G.t[:], ones_bf, ones_bf.t[r:r + 1, 0:128], lhi, lhi.t[r:r + 1, :], True, False)
    mm(PS_G, PS_G.t[:], ones_bf, ones_bf.t[r:r + 1, 0:128], llo, llo.t[r:r + 1, :], False, True)
    act(lbc, lbc.t[:], PS_G, PS_G.t[:], AF.Identity)
    tt("dve", tD, tD.t[:], pso, pso.t[:], lbc, lbc.t[:], ALU.mult)
    proj_fm(l, cgate, PS_G)
    act(gsil, gsil.t[:], PS_G, PS_G.t[:], AF.Silu, bias=bias_fm.t[:, cgate:cgate + 1], extra_reads=[bias_fm])
    tt("dve", out_t, out_ap, tD, tD.t[:], gsil, gsil.t[:], ALU.mult)


def tile_variant(ti, TPH, NT, dil):
    half = ti // TPH
    k = ti % TPH
    if k == 0:
        v = 1
    elif k == 1:
        v = 2
    elif k == TPH - 2:
        v = 3
    elif k == TPH - 1:
        v = 4
    else:
        return 0
    return v + 4 * half


VAR_TILE = None


def make_etab(HALF, conn):
    N = 2 * HALF
    TPH = HALF // TQ
    NT = N // TQ
    slopes = (2.0 ** (-8.0 * np.arange(1, 13) / 12)).reshape(3, 4)
    rep = {}
    for ti in range(NT):
        v = tile_variant(ti, TPH, NT, 16)
        rep.setdefault(v, ti)
    out = np.zeros((9, 3, 4, 2, 128, TQ), np.float32)
    j = np.arange(128)[:, None]
    for v, ti in rep.items():
        t0 = ti * TQ
        half = ti // TPH
        lo, hi = (0, N) if conn else (half * HALF, (half + 1) * HALF)
        for g, (window, dil) in enumerate(A_GROUPS):
            nq = TQ // dil
            col = np.arange(TQ)[None, :]
            if dil == 1:
                r = np.zeros_like(col)
                i0 = (col // 128) * 128
                i = col % 128
            else:
                r = col // nq
                i0 = np.zeros_like(col)
                i = col % nq
            for cc in range(2):
                kidx = i0 - 64 + 128 * cc + j
                qidx = i0 + i
                rel = kidx - qidx
                ktok = t0 + r + dil * kidx
                valid = (np.abs(rel) <= 64) & (ktok >= lo) & (ktok < hi)
                for hh in range(4):
                    e = np.exp(-slopes[g, hh] * dil * np.abs(rel).astype(np.float64))
                    out[v, g, hh, cc] = np.where(valid, e, 0.0)
    return out.astype(ml_dtypes.bfloat16)


def make_rope(HALF, conn):
    N = 2 * HALF
    t = np.arange(N)
    p = t if conn else t % HALF
    row = (p // GRID_W).astype(np.float32)
    colp = (p % GRID_W).astype(np.float32)
    axis_dim = HD // 2
    inv = (10000.0 ** (-np.arange(0, axis_dim, 2, dtype=np.float32) / axis_dim)).astype(np.float32)
    ar = row[:, None] * inv[None]
    ac = colp[:, None] * inv[None]
    ang = np.concatenate([ar, ar, ac, ac], -1)
    cos = np.cos(ang).astype(np.float32).T
    sin = np.sin(ang).astype(np.float32).T.copy()
    sin[0:32] *= -1.0
    sin[64:96] *= -1.0
    return np.ascontiguousarray(cos), np.ascontiguousarray(sin)


def make_perm():
    P = np.zeros((128, 128), np.float32)
    for m in range(128):
        k = m + 32 if (m % 64) < 32 else m - 32
        P[k, m] = 1.0
    return P


_NC_CACHE = {}


def run_cores(core_x, conns, weights, HALF):
    if HALF not in _NC_CACHE:
        _NC_CACHE[HALF] = build(HALF)
    nc = _NC_CACHE[HALF]
    perm = make_perm()
    consts = {}
    for conn in set(conns):
        cos, sin = make_rope(HALF, conn)
        consts[conn] = dict(cosT=cos, sinT=sin, etab=make_etab(HALF, conn),
                            cross=np.full((128, 1), 1.0 if conn else 0.0, np.float32))
    in_maps = []
    for x, conn in zip(core_x, conns):
        m = dict(weights)
        m["xT"] = np.ascontiguousarray(x.T)
        m["permM"] = perm
        m.update(consts[conn])
        in_maps.append(m)
    res = run_bass_kernel_spmd(nc, in_maps, core_ids=list(range(len(in_maps))))
    return [np.ascontiguousarray(r["yT"].T) for r in res.results]


def kernel(x_prompt, x_sample, w_in, b_in, q_gain, k_gain, w_proj_a, w_proj_b, w_out, ln_g, ln_b):
    f = lambda a: np.ascontiguousarray(np.asarray(a, dtype=np.float32))
    x_prompt, x_sample = f(x_prompt), f(x_sample)
    weights = dict(w_in=f(w_in), b_in=f(b_in), q_gain=f(q_gain), k_gain=f(k_gain), w_proj_a=f(w_proj_a),
                   w_proj_b=f(w_proj_b), w_out=f(w_out), ln_g=f(ln_g), ln_b=f(ln_b))
    HALF = x_prompt.shape[1]
    c0 = x_prompt[0:2].reshape(2 * HALF, D)
    c1 = x_prompt[2:4].reshape(2 * HALF, D)
    c2 = x_sample[0]
    xs = [c0, c1, c2]
    conns = [False, False, True]
    outs = run_cores(xs, conns, weights, HALF)
    y_prompt = np.concatenate([outs[0].reshape(2, HALF, D), outs[1].reshape(2, HALF, D)], 0)
    y_sample = outs[2].reshape(1, 2 * HALF, D)
    return (y_prompt.astype(np.float32), y_sample.astype(np.float32))
```

```python
import contextlib
import math
import numpy as np
import ml_dtypes
import concourse.bass as bass
import concourse.mybir as mybir
from concourse.bass_utils import run_bass_kernel_spmd

F32 = mybir.dt.float32
BF16 = mybir.dt.bfloat16
AF = mybir.ActivationFunctionType
ALU = mybir.AluOpType

D = 1024
HD = 128
INW = 9728
OFF_AQ, OFF_AK, OFF_AV, OFF_AG = 0, 1536, 3072, 4608
OFF_BQ, OFF_BK, OFF_BV, OFF_BG = 5120, 6144, 6400, 6656
OFF_GA, OFF_GB = 7680, 8704
DEPTH = 2
GRID_W = 64
A_GROUPS = ((128, 1), (512, 4), (2048, 16))
ALPHA = (2 * DEPTH) ** 0.25
RMS_EPS = 1e-6
LN_EPS = 1e-5
SCALE = 1.0 / math.sqrt(HD)
TQ = 512
PADL = 1024
PADR = 1024
NEG = -700.0
NDMA = 8


class Tl:
    __slots__ = ("t", "lw", "rd")

    def __init__(self, t):
        self.t = t
        self.lw = None
        self.rd = {}


class Sched:
    ENGS = ("pe", "act", "dve", "pool", "sp")

    def __init__(self):
        self.items = {e: [] for e in self.ENGS}
        self.cnt = {e: 0 for e in ("pe", "act", "dve", "pool")}
        self.dma_n = {"sp": 0, "pool": 0, "act": 0}
        self.dma_val = {}
        self.seen = {e: {} for e in self.ENGS}

    def _deps(self, eng, reads, writes):
        deps = {}

        def add(k, v):
            if deps.get(k, 0) < v:
                deps[k] = v
        for t in reads:
            if t.lw is not None:
                add(*t.lw)
        for t in writes:
            if t.lw is not None:
                add(*t.lw)
            for k, v in t.rd.items():
                add(k, v)
        waits = []
        for k, v in deps.items():
            if k == "pe" and eng == "pe":
                continue
            if self.seen[eng].get(k, 0) >= v:
                continue
            self.seen[eng][k] = v
            waits.append((k, v))
        return waits

    def _mark(self, me, reads, writes):
        k, v = me
        for t in reads:
            if t.rd.get(k, 0) < v:
                t.rd[k] = v
        for t in writes:
            t.lw = me
            t.rd = {}

    def op(self, eng, emit, reads=(), writes=()):
        waits = self._deps(eng, reads, writes)
        self.cnt[eng] += 1
        me = (eng, self.cnt[eng])
        self._mark(me, reads, writes)
        self.items[eng].append((waits, emit, eng, 1))

    def dma(self, q, out, in_, reads=(), writes=(), slow=False):
        n = self.dma_n[q]
        self.dma_n[q] = n + 1
        key = ("dma", q, n % NDMA)
        prev = self.dma_val.get(key, 0)
        waits = self._deps(q, reads, writes)
        if prev and self.seen[q].get(key, 0) < prev:
            self.seen[q][key] = prev
            waits.append((key, prev))
        val = prev + 16
        self.dma_val[key] = val
        self._mark((key, val), reads, writes)
        if slow:
            emit = lambda e: e.dma_start(out=out, in_=in_, allow_slow_non_contiguous=True)
        else:
            emit = lambda e: e.dma_start(out=out, in_=in_)
        self.items[q].append((waits, emit, key, 16))

    def barrier(self):
        allv = dict(self.cnt)
        allv.update(self.dma_val)
        for e in self.ENGS:
            waits = []
            for k, v in allv.items():
                if v and k != e and self.seen[e].get(k, 0) < v:
                    self.seen[e][k] = v
                    waits.append((k, v))
            if waits:
                self.items[e].append((waits, None, None, 0))


def build(HALF):
    N = 2 * HALF
    NT = N // TQ
    TPH = HALF // TQ
    nc = bass.Bass("TRN2", target_bir_lowering=False)
    S = Sched()
    es = contextlib.ExitStack()

    def dram(name, shape, dt, kind="Internal"):
        return nc.dram_tensor(name, list(shape), dt, kind=kind)

    xT = dram("xT", [D, N], F32, "ExternalInput")
    w_in = dram("w_in", [DEPTH, D, INW], F32, "ExternalInput")
    b_in = dram("b_in", [DEPTH, INW], F32, "ExternalInput")
    q_gain = dram("q_gain", [DEPTH, HD], F32, "ExternalInput")
    k_gain = dram("k_gain", [DEPTH, HD], F32, "ExternalInput")
    w_pa = dram("w_proj_a", [DEPTH, 512, D], F32, "ExternalInput")
    w_pb = dram("w_proj_b", [DEPTH, D, D], F32, "ExternalInput")
    w_o = dram("w_out", [DEPTH, D, D], F32, "ExternalInput")
    ln_g = dram("ln_g", [DEPTH, D], F32, "ExternalInput")
    ln_b = dram("ln_b", [DEPTH, D], F32, "ExternalInput")
    cosT = dram("cosT", [HD, N], F32, "ExternalInput")
    sinT = dram("sinT", [HD, N], F32, "ExternalInput")
    permM = dram("permM", [HD, HD], F32, "ExternalInput")
    cross = dram("cross", [128, 1], F32, "ExternalInput")
    NVAR = 9
    etab = dram("etab", [NVAR, 3, 4, 2, 128, TQ], BF16, "ExternalInput")
    yT = dram("yT", [D, N], F32, "ExternalOutput")
    wbf = dram("wbf", [DEPTH, INW // 128, 128, 8 * 128], BF16)
    wpabf = dram("wpabf", [DEPTH, 128, 4 * D], BF16)
    wpbbf = dram("wpbbf", [DEPTH, 128, 8 * D], BF16)
    wobf = dram("wobf", [DEPTH, 128, 8 * D], BF16)
    x1T = dram("x1T", [D, N], F32)
    NPAD = PADL + N + PADR
    akT = dram("akT", [12, 128, NPAD], BF16)
    av = dram("av", [NPAD, 1536], BF16)
    kbT = dram("kbT", [2, 128, N], BF16)
    vb = dram("vb", [N, 256], BF16)

    def sb(name, shape, dt):
        return Tl(es.enter_context(nc.sbuf_tensor(name, list(shape), dt)))

    def psum(name):
        return Tl(es.enter_context(nc.psum_tensor(name, [128, TQ], F32)))

    ones_bf = sb("ones_bf", [128, 512], BF16)
    zero_bf = sb("zero_bf", [128, 1536], BF16)
    perm_f = sb("perm_f", [128, 128], F32)
    perm_b = sb("perm_b", [128, 128], BF16)
    cross_t = sb("cross_t", [128, 1], F32)
    cross_bf = sb("cross_bf", [128, 1], BF16)
    eps_rms = sb("eps_rms", [128, 1], F32)
    eps_ln = sb("eps_ln", [128, 1], F32)
    bias_fm = sb("bias_fm", [128, INW // 128], F32)
    bias_row = sb("bias_row", [1, 1792], BF16)
    qg_t = sb("qg_t", [128, 1], F32)
    kg_t = sb("kg_t", [128, 1], F32)
    lng_t = sb("lng_t", [128, 8], F32)
    lnb_t = sb("lnb_t", [128, 8], F32)
    xf = [sb(f"xf{i}", [128, TQ], F32) for i in range(2)]
    xb = sb("xb", [128, 8, TQ], BF16)
    wstage = [sb(f"wstage{i}", [128, 8, 128], F32) for i in range(1)]
    wt = [sb(f"wt{i}", [128, 8, 128], BF16) for i in range(3)]
    wpa_t = sb("wpa_t", [128, 4, D], BF16)
    wpb_t = sb("wpb_t", [128, 8, D], BF16)
    wo_t = sb("wo_t", [128, 8, D], BF16)
    cos_t = sb("cos_t", [128, TQ], F32)
    sin_t = sb("sin_t", [128, TQ], F32)
    tA = sb("tA", [128, TQ], F32)
    tB = sb("tB", [128, TQ], F32)
    tC = sb("tC", [128, TQ], F32)
    tD = sb("tD", [128, TQ], F32)
    tbf = sb("tbf", [128, TQ], BF16)
    tbf2 = sb("tbf2", [128, TQ], BF16)
    stg = [sb(f"stg{i}", [128, TQ], BF16) for i in range(2)]
    qb = [sb(f"qb{i}", [128, TQ], BF16) for i in range(4)]
    gsil = sb("gsil", [128, TQ], F32)
    yb = sb("yb", [128, 8, TQ], BF16)
    ya = sb("ya", [128, 4, TQ], BF16)
    Pt = [sb(f"Pt{i}", [128, TQ], BF16) for i in range(3)]
    Pm = [sb(f"Pm{i}", [128, TQ], BF16) for i in range(2)]
    Et = [sb(f"Et{i}", [128, TQ], BF16) for i in range(4)]
    Kg = [sb(f"Kg{i}", [128, 2048], BF16) for i in range(2)]
    Vg = [sb(f"Vg{i}", [128, 16, 128], BF16) for i in range(2)]
    Kw = [sb(f"Kw{i}", [128, 2560], BF16) for i in range(2)]
    Vw = [sb(f"Vw{i}", [128, 32, 128], BF16) for i in range(2)]
    qa = [sb(f"qa{i}", [128, TQ], BF16) for i in range(2)]
    lrow = sb("lrow", [128, TQ], F32)
    lhi = sb("lhi", [128, TQ], BF16)
    llo = sb("llo", [128, TQ], BF16)
    lbc = sb("lbc", [128, TQ], F32)
    mrg = sb("mrg", [128, 8, TQ], BF16)
    z = sb("z", [128, 8, TQ], F32)
    yout = [sb(f"yout{i}", [128, TQ], F32) for i in range(2)]
    ps = [psum(f"ps{i}") for i in range(8)]
    PS_S = [ps[0], ps[1]]
    PS_O = [ps[2], ps[3], ps[4], ps[5]]
    PS_L = ps[6]
    PS_G = ps[7]

    def mm(out_t, out_ap, lhs_t, lhs_ap, rhs_t, rhs_ap, start, stop):
        S.op("pe", lambda e: e.matmul(out_ap, lhsT=lhs_ap, rhs=rhs_ap, start=start, stop=stop,
                                      skip_group_check=True),
             reads=[lhs_t, rhs_t] + ([] if start else [out_t]), writes=[out_t])

    def act(out_t, out_ap, in_t, in_ap, func, bias=0.0, scale=1.0, extra_reads=()):
        S.op("act", lambda e: e.activation(out=out_ap, in_=in_ap, func=func, bias=bias, scale=scale),
             reads=[in_t] + list(extra_reads), writes=[out_t])

    def tt(eng, out_t, out_ap, a_t, a_ap, b_t, b_ap, op):
        S.op(eng, lambda e: e.tensor_tensor(out=out_ap, in0=a_ap, in1=b_ap, op=op),
             reads=[a_t, b_t], writes=[out_t])

    def ts(eng, out_t, out_ap, a_t, a_ap, s1, s2, op0, op1, extra_reads=()):
        if s2 is None:
            S.op(eng, lambda e: e.tensor_scalar(out=out_ap, in0=a_ap, scalar1=s1, scalar2=None, op0=op0),
                 reads=[a_t] + list(extra_reads), writes=[out_t])
        else:
            S.op(eng, lambda e: e.tensor_scalar(out=out_ap, in0=a_ap, scalar1=s1, scalar2=s2, op0=op0, op1=op1),
                 reads=[a_t] + list(extra_reads), writes=[out_t])

    def stt(eng, out_t, out_ap, a_t, a_ap, sc, b_t, b_ap, op0, op1, extra_reads=()):
        S.op(eng, lambda e: e.scalar_tensor_tensor(out=out_ap, in0=a_ap, scalar=sc, in1=b_ap, op0=op0, op1=op1),
             reads=[a_t, b_t] + list(extra_reads), writes=[out_t])

    def cp(eng, out_t, out_ap, in_t, in_ap):
        S.op(eng, lambda e: e.tensor_copy(out=out_ap, in_=in_ap), reads=[in_t], writes=[out_t])

    def mset(eng, t, ap, val):
        S.op(eng, lambda e: e.memset(ap, val), writes=[t])

    mset("dve", ones_bf, ones_bf.t[:], 1.0)
    mset("dve", zero_bf, zero_bf.t[:], 0.0)
    mset("dve", eps_rms, eps_rms.t[:], RMS_EPS)
    mset("dve", eps_ln, eps_ln.t[:], LN_EPS)
    S.dma("sp", perm_f.t[:], permM[:, :], writes=[perm_f])
    cp("dve", perm_b, perm_b.t[:], perm_f, perm_f.t[:])
    S.dma("sp", cross_t.t[:], cross[:, :], writes=[cross_t])
    cp("dve", cross_bf, cross_bf.t[:], cross_t, cross_t.t[:])
    for h in range(12):
        S.dma("pool", akT[h, :, 0:PADL], zero_bf.t[:, 0:PADL], reads=[zero_bf])
        S.dma("pool", akT[h, :, PADL + N:NPAD], zero_bf.t[:, 0:PADR], reads=[zero_bf])
    for r0 in list(range(0, PADL, 128)) + list(range(PADL + N, NPAD, 128)):
        S.dma("pool", av[r0:r0 + 128, :], zero_bf.t[:, :], reads=[zero_bf])

    ci = 0
    for l in range(DEPTH):
        for c in range(INW // 128):
            st = wstage[0]
            w16 = wt[ci % 3]
            S.dma("sp", st.t[:], w_in[l, :, c * 128:(c + 1) * 128].rearrange("(kc p) j -> p kc j", p=128),
                  writes=[st])
            cp("dve" if ci % 2 == 0 else "pool", w16, w16.t[:], st, st.t[:])
            S.dma("pool", wbf[l, c].rearrange("p (kc j) -> p kc j", j=128), w16.t[:], reads=[w16])
            ci += 1
        for (src, dst, nk) in ((w_pa, wpabf, 4), (w_pb, wpbbf, 8), (w_o, wobf, 8)):
            for oc in range(8):
                st = wstage[0]
                w16 = wt[ci % 3]
                S.dma("sp", st.t[:, 0:nk, :],
                      src[l, :, oc * 128:(oc + 1) * 128].rearrange("(kc p) j -> p kc j", p=128), writes=[st])
                cp("dve" if ci % 2 == 0 else "pool", w16, w16.t[:, 0:nk, :], st, st.t[:, 0:nk, :])
                S.dma("pool", dst[l].rearrange("p (kc j) -> p kc j", j=D)[:, :, oc * 128:(oc + 1) * 128],
                      w16.t[:, 0:nk, :], reads=[w16])
                ci += 1
    S.barrier()
    import os
    STOP = int(os.environ.get("KSTOP", "99"))

    wcnt = [0]

    def load_w(l, c):
        w16 = wt[wcnt[0] % 3]
        wcnt[0] += 1
        S.dma("sp", w16.t[:], wbf[l, c].rearrange("p (kc j) -> p kc j", j=128), writes=[w16])
        return w16

    def proj_fm(l, c, pst):
        w16 = load_w(l, c)
        for kc in range(8):
            mm(pst, pst.t[:], w16, w16.t[:, kc, :], xb, xb.t[:, kc, :], kc == 0, kc == 7)

    xcnt = [0]

    def load_x(l, t0):
        src = xT if l == 0 else x1T
        for kc in range(8):
            xs = xf[xcnt[0] % 2]
            xcnt[0] += 1
            S.dma("sp", xs.t[:], src[kc * 128:(kc + 1) * 128, t0:t0 + TQ], writes=[xs])
            cp("pool" if kc % 2 else "dve", xb, xb.t[:, kc, :], xs, xs.t[:])

    def load_rope(t0):
        S.dma("sp", cos_t.t[:], cosT[:, t0:t0 + TQ], writes=[cos_t])
        S.dma("sp", sin_t.t[:], sinT[:, t0:t0 + TQ], writes=[sin_t])

    def norm_rope(pst, c, gain_t, out_t, out_ap):
        act(tA, tA.t[:], pst, pst.t[:], AF.Identity, bias=bias_fm.t[:, c:c + 1], extra_reads=[bias_fm])
        tt("dve", tbf, tbf.t[:], tA, tA.t[:], tA, tA.t[:], ALU.mult)
        mm(PS_G, PS_G.t[:], ones_bf, ones_bf.t[:, 0:128], tbf, tbf.t[:], True, True)
        act(tB, tB.t[:], PS_G, PS_G.t[:], AF.Sqrt, bias=eps_rms.t[:, 0:1], scale=1.0 / HD, extra_reads=[eps_rms])
        S.op("dve", lambda e: e.reciprocal(out=tB.t[:], in_=tB.t[:]), reads=[tB], writes=[tB])
        stt("dve", tC, tC.t[:], tA, tA.t[:], gain_t.t[:, 0:1], tB, tB.t[:], ALU.mult, ALU.mult,
            extra_reads=[gain_t])
        cp("dve", tbf2, tbf2.t[:], tC, tC.t[:])
        mm(PS_G, PS_G.t[:], perm_b, perm_b.t[:], tbf2, tbf2.t[:], True, True)
        tt("dve", tD, tD.t[:], PS_G, PS_G.t[:], sin_t, sin_t.t[:], ALU.mult)
        tt("dve", tC, tC.t[:], tC, tC.t[:], cos_t, cos_t.t[:], ALU.mult)
        tt("dve", out_t, out_ap, tC, tC.t[:], tD, tD.t[:], ALU.add)

    vcols = [(OFF_BV // 128 + i, "b", i) for i in range(2)] + [(OFF_AV // 128 + i, "a", i) for i in range(12)]

    for l in range(DEPTH):
        if STOP <= 2 * l:
            break
        for c in range(INW // 128):
            S.dma("sp", bias_fm.t[:, c:c + 1], b_in[l, c * 128:(c + 1) * 128].rearrange("(p o) -> p o", o=1),
                  writes=[bias_fm], slow=True)
        S.dma("sp", z.t[0:1, 0, 0:256], b_in[l:l + 1, OFF_BV:OFF_BV + 256], writes=[z])
        S.dma("sp", z.t[0:1, 1:4, :], b_in[l:l + 1, OFF_AV:OFF_AV + 1536].rearrange("o (a b) -> o a b", b=512),
              writes=[z])
        cp("dve", bias_row, bias_row.t[0:1, 0:256], z, z.t[0:1, 0, 0:256])
        cp("dve", bias_row, bias_row.t[0:1, 256:1792].rearrange("o (a b) -> o a b", b=512), z, z.t[0:1, 1:4, :])
        S.dma("sp", qg_t.t[:], q_gain[l].rearrange("(p o) -> p o", o=1), writes=[qg_t], slow=True)
        S.dma("sp", kg_t.t[:], k_gain[l].rearrange("(p o) -> p o", o=1), writes=[kg_t], slow=True)
        for c in range(8):
            S.dma("sp", lng_t.t[:, c:c + 1], ln_g[l, c * 128:(c + 1) * 128].rearrange("(p o) -> p o", o=1),
                  writes=[lng_t], slow=True)
            S.dma("sp", lnb_t.t[:, c:c + 1], ln_b[l, c * 128:(c + 1) * 128].rearrange("(p o) -> p o", o=1),
                  writes=[lnb_t], slow=True)
        S.dma("sp", wpa_t.t[:], wpabf[l].rearrange("p (kc j) -> p kc j", j=D), writes=[wpa_t])
        S.dma("sp", wpb_t.t[:], wpbbf[l].rearrange("p (kc j) -> p kc j", j=D), writes=[wpb_t])
        S.dma("sp", wo_t.t[:], wobf[l].rearrange("p (kc j) -> p kc j", j=D), writes=[wo_t])

        sc = 0
        for ti in range(NT):
            t0 = ti * TQ
            load_x(l, t0)
            load_rope(t0)
            for kvh in range(2):
                c = OFF_BK // 128 + kvh
                proj_fm(l, c, PS_S[kvh])
                so = stg[sc % 2]
                sc += 1
                norm_rope(PS_S[kvh], c, kg_t, so, so.t[:])
                S.dma("pool", kbT[kvh, :, t0:t0 + TQ], so.t[:], reads=[so])
            for h in range(12):
                c = OFF_AK // 128 + h
                pst = PS_O[h % 4]
                proj_fm(l, c, pst)
                so = stg[sc % 2]
                sc += 1
                act(so, so.t[:], pst, pst.t[:], AF.Identity, bias=bias_fm.t[:, c:c + 1], extra_reads=[bias_fm])
                S.dma("pool", akT[h, :, PADL + t0:PADL + t0 + TQ], so.t[:], reads=[so])
            for vi, (c, which, i) in enumerate(vcols):
                w16 = load_w(l, c)
                pst = PS_O[vi % 4]
                for tc in range(4):
                    for kc in range(8):
                        mm(pst, pst.t[:, tc * 128:(tc + 1) * 128], xb, xb.t[:, kc, tc * 128:(tc + 1) * 128],
                           w16, w16.t[:, kc, :], kc == 0 and tc == 0, False)
                    mm(pst, pst.t[:, tc * 128:(tc + 1) * 128], ones_bf, ones_bf.t[0:1, 0:128],
                       bias_row, bias_row.t[0:1, vi * 128:(vi + 1) * 128], False, True)
                so = stg[sc % 2]
                sc += 1
                if vi % 2 == 0:
                    act(so, so.t[:], pst, pst.t[:], AF.Identity)
                else:
                    cp("dve", so, so.t[:], pst, pst.t[:])
                if which == "b":
                    dst = vb[t0:t0 + TQ, i * 128:(i + 1) * 128]
                else:
                    dst = av[PADL + t0:PADL + t0 + TQ, i * 128:(i + 1) * 128]
                S.dma("pool", dst.rearrange("(tc p) j -> p tc j", p=128),
                      so.t[:].rearrange("p (tc j) -> p tc j", j=128), reads=[so])
        S.barrier()
        if STOP <= 2 * l + 1:
            break

        kvc = 0
        pc = 0
        awc = 0
        for ti in range(NT):
            t0 = ti * TQ
            qhalf = ti // TPH
            load_x(l, t0)
            load_rope(t0)
            for kvh in range(2):
                for a in range(4):
                    h = 4 * kvh + a
                    c = OFF_BQ // 128 + h
                    proj_fm(l, c, PS_S[a % 2])
                    norm_rope(PS_S[a % 2], c, qg_t, qb[a], qb[a].t[:])
                GK = min(2048, HALF)
                ngrp = N // GK
                cpg = GK // 128
                grp_buf = {}

                def load_grp(grp):
                    nonlocal kvc
                    kg = Kg[kvc % 2]
                    vg = Vg[kvc % 2]
                    kvc += 1
                    k0 = grp * GK
                    S.dma("sp", kg.t[:, 0:GK], kbT[kvh, :, k0:k0 + GK], writes=[kg])
                    S.dma("sp", vg.t[:, 0:cpg, :], vb[k0:k0 + GK, kvh * 128:(kvh + 1) * 128].rearrange(
                        "(m p) j -> p m j", p=128), writes=[vg])
                    is_cross = (k0 // HALF) != qhalf
                    if is_cross:
                        vflat = vg.t[:, 0:cpg, :].rearrange("p m j -> p (m j)")
                        ts("dve", vg, vflat, vg, vflat, cross_t.t[:, 0:1], 0.0, ALU.mult, ALU.add,
                           extra_reads=[cross_t])
                    grp_buf[grp] = (kg, vg, cross_bf if is_cross else ones_bf)

                units = [(grp, m, a) for grp in range(ngrp) for m in range(cpg) for a in range(4)]
                NU = len(units)
                ubuf = {}

                def issue_qk(u):
                    nonlocal pc
                    grp, m, a = units[u]
                    kg = grp_buf[grp][0]
                    pss = PS_S[pc % 2]
                    p_t = Pt[pc % 3]
                    pc += 1
                    ubuf[u] = p_t
                    mm(pss, pss.t[:], kg, kg.t[:, m * 128:(m + 1) * 128], qb[a], qb[a].t[:], True, True)
                    act(p_t, p_t.t[:], pss, pss.t[:], AF.Exp, scale=SCALE)

                def issue_pv(u):
                    grp, m, a = units[u]
                    kg, vg, lcol_t = grp_buf[grp]
                    p_t = ubuf.pop(u)
                    first = (grp == 0 and m == 0)
                    last = (grp == ngrp - 1 and m == cpg - 1)
                    mm(PS_O[a], PS_O[a].t[:], vg, vg.t[:, m, :], p_t, p_t.t[:], first, last)
                    lt_, lr_ = (PS_L, 32 * a) if a < 3 else (PS_G, 0)
                    mm(lt_, lt_.t[lr_:lr_ + 1, :], lcol_t, lcol_t.t[:, 0:1], p_t, p_t.t[:], first, last)

                load_grp(0)
                if ngrp > 1:
                    load_grp(1)
                issue_qk(0)
                issue_qk(1)
                for u in range(NU):
                    if u + 2 < NU:
                        issue_qk(u + 2)
                    issue_pv(u)
                    grp, m, a = units[u]
                    if m == cpg - 1 and a == 3 and grp + 2 < ngrp:
                        load_grp(grp + 2)
                for a in (3, 0, 1, 2):
                    h = 4 * kvh + a
                    lt_, lr_ = (PS_L, 32 * a) if a < 3 else (PS_G, 0)
                    finish_head(S, locals(), PS_O[a], lt_, lr_, yb, yb.t[:, h, :], l, OFF_BG // 128 + h)
            combos = [(hh, g) for hh in range(4) for g in range(3)]

            def prepA(ci):
                nonlocal awc
                hh, g = combos[ci]
                dil = A_GROUPS[g][1]
                ah = 4 * g + hh
                c = OFF_AQ // 128 + ah
                qt = qa[awc % 2]
                kw = Kw[awc % 2]
                vw = Vw[awc % 2]
                ets = (Et[(2 * awc) % 4], Et[(2 * awc + 1) % 4])
                awc += 1
                wlen = {1: 640, 4: 1024, 16: 2560}[dil]
                w0 = PADL + t0 - 64 * dil
                S.dma("sp", kw.t[:, 0:wlen], akT[ah, :, w0:w0 + wlen], writes=[kw])
                nq = TQ // dil
                if dil == 1:
                    S.dma("sp", vw.t[:, 0:5, :], av[w0:w0 + 640, ah * 128:(ah + 1) * 128].rearrange(
                        "(m p) j -> p m j", p=128), writes=[vw])
                    nk1 = 128
                else:
                    nk1 = 128 if dil == 4 else 32
                    for cc in range(2):
                        nk = 128 if cc == 0 else nk1
                        base = w0 + dil * 128 * cc
                        S.dma("sp", vw.t[0:nk, cc * dil:(cc + 1) * dil, :],
                              av[base:base + dil * nk, ah * 128:(ah + 1) * 128].rearrange(
                                  "(p r) j -> p r j", r=dil), writes=[vw])
                var = tile_variant(ti, TPH, NT, dil)
                for cc in range(2):
                    S.dma("sp", ets[cc].t[:], etab[var, g, hh, cc], writes=[ets[cc]])
                proj_fm(l, c, PS_G)
                if dil == 1:
                    act(qt, qt.t[:], PS_G, PS_G.t[:], AF.Identity, bias=bias_fm.t[:, c:c + 1],
                        extra_reads=[bias_fm])
                else:
                    act(qt, qt.t[:].rearrange("p (r i) -> p r i", r=dil), PS_G,
                        PS_G.t[:].rearrange("p (i r) -> p r i", r=dil), AF.Identity,
                        bias=bias_fm.t[:, c:c + 1], extra_reads=[bias_fm])
                return (qt, kw, vw, ets, nk1)

            def runA(ci, pre):
                nonlocal pc
                hh, g = combos[ci]
                dil = A_GROUPS[g][1]
                qt, kw, vw, ets, nk1 = pre
                nq = TQ // dil
                if dil == 1:
                    blocks = [(b * 128, 128, 0, b * 128) for b in range(4)]
                    vidx = lambda bi, cc: bi + cc
                else:
                    blocks = [(r * nq, nq, r, 0) for r in range(dil)]
                    vidx = lambda bi, cc, dil=dil: cc * dil + bi
                for cc in range(2):
                    nk = 128 if cc == 0 else nk1
                    pss = PS_S[pc % 2]
                    p_t = Pt[pc % 3]
                    pm_t = Pm[pc % 2]
                    e_t = ets[cc]
                    pc += 1
                    for bi, (c0, ncol, r, i0) in enumerate(blocks):
                        off = r + dil * (i0 + 128 * cc)
                        lhs = kw.t[:, off:off + (nk - 1) * dil + 1:dil] if dil > 1 else kw.t[:, off:off + nk]
                        mm(pss, pss.t[0:nk, c0:c0 + ncol], kw, lhs, qt, qt.t[:, c0:c0 + ncol], True, True)
                    act(p_t, p_t.t[0:nk, :], pss, pss.t[0:nk, :], AF.Exp, scale=SCALE)
                    tt("pool", pm_t, pm_t.t[0:nk, :], p_t, p_t.t[0:nk, :], e_t, e_t.t[0:nk, :], ALU.mult)
                    for bi, (c0, ncol, r, i0) in enumerate(blocks):
                        if dil == 1:
                            ocols = slice(c0, c0 + ncol)
                        else:
                            ocols = slice(r, r + (ncol - 1) * dil + 1, dil)
                        firstA = (g == 0 and cc == 0 and bi == 0)
                        lastA = (g == 2 and cc == 1 and bi == len(blocks) - 1)
                        mm(PS_O[0], PS_O[0].t[:, ocols], vw, vw.t[0:nk, vidx(bi, cc), :],
                           pm_t, pm_t.t[0:nk, c0:c0 + ncol], firstA, lastA)
                        mm(PS_L, PS_L.t[0:1, ocols], ones_bf, ones_bf.t[0:nk, 0:1],
                           pm_t, pm_t.t[0:nk, c0:c0 + ncol], firstA, lastA)

            preA = prepA(0)
            for ci in range(12):
                nxtA = prepA(ci + 1) if ci + 1 < 12 else None
                runA(ci, preA)
                preA = nxtA
                hh, g = combos[ci]
                if g == 2:
                    finish_head(S, locals(), PS_O[0], PS_L, 0, ya, ya.t[:, hh, :], l, OFF_AG // 128 + hh)
            for oc in range(8):
                pa = PS_O[1]
                pbp = PS_O[2]
                for kc in range(4):
                    mm(pa, pa.t[:], wpa_t, wpa_t.t[:, kc, oc * 128:(oc + 1) * 128], ya, ya.t[:, kc, :],
                       kc == 0, kc == 3)
                for kc in range(8):
                    mm(pbp, pbp.t[:], wpb_t, wpb_t.t[:, kc, oc * 128:(oc + 1) * 128], yb, yb.t[:, kc, :],
                       kc == 0, kc == 7)
                cga = OFF_GA // 128 + oc
                proj_fm(l, cga, PS_S[0])
                act(tA, tA.t[:], PS_S[0], PS_S[0].t[:], AF.Sigmoid, bias=bias_fm.t[:, cga:cga + 1],
                    extra_reads=[bias_fm])
                cgb = OFF_GB // 128 + oc
                proj_fm(l, cgb, PS_S[1])
                act(tB, tB.t[:], PS_S[1], PS_S[1].t[:], AF.Sigmoid, bias=bias_fm.t[:, cgb:cgb + 1],
                    extra_reads=[bias_fm])
                tt("dve", tC, tC.t[:], pa, pa.t[:], tA, tA.t[:], ALU.mult)
                tt("dve", tD, tD.t[:], pbp, pbp.t[:], tB, tB.t[:], ALU.mult)
                tt("dve", mrg, mrg.t[:, oc, :], tC, tC.t[:], tD, tD.t[:], ALU.add)
            src = xT if l == 0 else x1T
            for oc in range(8):
                pso = PS_O[oc % 2 + 1]
                for kc in range(8):
                    mm(pso, pso.t[:], wo_t, wo_t.t[:, kc, oc * 128:(oc + 1) * 128], mrg, mrg.t[:, kc, :],
                       kc == 0, kc == 7)
                xs = xf[xcnt[0] % 2]
                xcnt[0] += 1
                S.dma("sp", xs.t[:], src[oc * 128:(oc + 1) * 128, t0:t0 + TQ], writes=[xs])
                stt("dve", z, z.t[:, oc, :], xs, xs.t[:], ALPHA, pso, pso.t[:], ALU.mult, ALU.add)
                cp("pool", tbf if oc % 2 == 0 else tbf2, (tbf if oc % 2 == 0 else tbf2).t[:], z, z.t[:, oc, :])
                zb = tbf if oc % 2 == 0 else tbf2
                mm(PS_G, PS_G.t[:], ones_bf, ones_bf.t[:, 0:128], zb, zb.t[:], oc == 0, oc == 7)
            ts("dve", tA, tA.t[:], PS_G, PS_G.t[:], 1.0 / D, 0.0, ALU.mult, ALU.add)
            for oc in range(8):
                tt("dve", z, z.t[:, oc, :], z, z.t[:, oc, :], tA, tA.t[:], ALU.subtract)
                zb = tbf if oc % 2 == 0 else tbf2
                tt("pool", zb, zb.t[:], z, z.t[:, oc, :], z, z.t[:, oc, :], ALU.mult)
                mm(PS_S[0], PS_S[0].t[:], ones_bf, ones_bf.t[:, 0:128], zb, zb.t[:], oc == 0, oc == 7)
            act(tB, tB.t[:], PS_S[0], PS_S[0].t[:], AF.Sqrt, bias=eps_ln.t[:, 0:1], scale=1.0 / D, extra_reads=[eps_ln])
            S.op("dve", lambda e: e.reciprocal(out=tB.t[:], in_=tB.t[:]), reads=[tB], writes=[tB])
            dst = x1T if l == 0 else yT
            for oc in range(8):
                yo = yout[oc % 2]
                tt("dve", tC, tC.t[:], z, z.t[:, oc, :], tB, tB.t[:], ALU.mult)
                ts("dve", yo, yo.t[:], tC, tC.t[:], lng_t.t[:, oc:oc + 1], lnb_t.t[:, oc:oc + 1],
                   ALU.mult, ALU.add, extra_reads=[lng_t, lnb_t])
                S.dma("pool", dst[oc * 128:(oc + 1) * 128, t0:t0 + TQ], yo.t[:], reads=[yo])
        S.barrier()

    sem_names = {}
    keys = list(S.cnt.keys()) + list(S.dma_val.keys())
    for k in keys:
        nm = k if isinstance(k, str) else f"d_{k[1]}_{k[2]}"
        sem_names[k] = es.enter_context(nc.semaphore("s_" + nm))
    block = es.enter_context(nc.Block())

    def replay(engname):
        def run(e):
            for waits, emit, key, inc in S.items[engname]:
                for k, v in waits:
                    e.wait_ge(sem_names[k], v)
                if emit is not None:
                    emit(e).then_inc(sem_names[key], inc)
        return run

    block.tensor(replay("pe"))
    block.scalar(replay("act"))
    block.vector(replay("dve"))
    block.gpsimd(replay("pool"))
    block.sync(replay("sp"))
    es.close()
    return nc


def finish_head(S, env, pso, L_t, r, out_t, out_ap, l, cgate):
    PS_L, PS_G = L_t, env["PS_G"]
    lrow, lhi, llo, lbc = env["lrow"], env["lhi"], env["llo"], env["lbc"]
    ones_bf, gsil, bias_fm, tD = env["ones_bf"], env["gsil"], env["bias_fm"], env["tD"]
    act, cp, tt, mm, proj_fm = env["act"], env["cp"], env["tt"], env["mm"], env["proj_fm"]
    S.op("dve", lambda e: e.reciprocal(out=lrow.t[r:r + 1, :], in_=PS_L.t[r:r + 1, :]), reads=[PS_L], writes=[lrow])
    cp("dve", lhi, lhi.t[r:r + 1, :], lrow, lrow.t[r:r + 1, :])
    tt("dve", lrow, lrow.t[r:r + 1, :], lrow, lrow.t[r:r + 1, :], lhi, lhi.t[r:r + 1, :], ALU.subtract)
    cp("dve", llo, llo.t[r:r + 1, :], lrow, lrow.t[r:r + 1, :])
    mm(PS_G, PS_G.t[:], ones_bf, ones_bf.t[r:r + 1, 0:128], lhi, lhi.t[r:r + 1, :], True, False)
    mm(PS_G, PS_G.t[:], ones_bf, ones_bf.t[r:r + 1, 0:128], llo, llo.t[r:r + 1, :], False, True)
    act(lbc, lbc.t[:], PS_G, PS_G.t[:], AF.Identity)
    tt("dve", tD, tD.t[:], pso, pso.t[:], lbc, lbc.t[:], ALU.mult)
    proj_fm(l, cgate, PS_G)
    act(gsil, gsil.t[:], PS_G, PS_G.t[:], AF.Silu, bias=bias_fm.t[:, cgate:cgate + 1], extra_reads=[bias_fm])
    tt("dve", out_t, out_ap, tD, tD.t[:], gsil, gsil.t[:], ALU.mult)


def tile_variant(ti, TPH, NT, dil):
    half = ti // TPH
    k = ti % TPH
    if k == 0:
        v = 1
    elif k == 1:
        v = 2
    elif k == TPH - 2:
        v = 3
    elif k == TPH - 1:
        v = 4
    else:
        return 0
    return v + 4 * half


VAR_TILE = None


def make_etab(HALF, conn):
    N = 2 * HALF
    TPH = HALF // TQ
    NT = N // TQ
    slopes = (2.0 ** (-8.0 * np.arange(1, 13) / 12)).reshape(3, 4)
    rep = {}
    for ti in range(NT):
        v = tile_variant(ti, TPH, NT, 16)
        rep.setdefault(v, ti)
    out = np.zeros((9, 3, 4, 2, 128, TQ), np.float32)
    j = np.arange(128)[:, None]
    for v, ti in rep.items():
        t0 = ti * TQ
        half = ti // TPH
        lo, hi = (0, N) if conn else (half * HALF, (half + 1) * HALF)
        for g, (window, dil) in enumerate(A_GROUPS):
            nq = TQ // dil
            col = np.arange(TQ)[None, :]
            if dil == 1:
                r = np.zeros_like(col)
                i0 = (col // 128) * 128
                i = col % 128
            else:
                r = col // nq
                i0 = np.zeros_like(col)
                i = col % nq
            for cc in range(2):
                kidx = i0 - 64 + 128 * cc + j
                qidx = i0 + i
                rel = kidx - qidx
                ktok = t0 + r + dil * kidx
                valid = (np.abs(rel) <= 64) & (ktok >= lo) & (ktok < hi)
                for hh in range(4):
                    e = np.exp(-slopes[g, hh] * dil * np.abs(rel).astype(np.float64))
                    out[v, g, hh, cc] = np.where(valid, e, 0.0)
    return out.astype(ml_dtypes.bfloat16)


def make_rope(HALF, conn):
    N = 2 * HALF
    t = np.arange(N)
    p = t if conn else t % HALF
    row = (p // GRID_W).astype(np.float32)
    colp = (p % GRID_W).astype(np.float32)
    axis_dim = HD // 2
    inv = (10000.0 ** (-np.arange(0, axis_dim, 2, dtype=np.float32) / axis_dim)).astype(np.float32)
    ar = row[:, None] * inv[None]
    ac = colp[:, None] * inv[None]
    ang = np.concatenate([ar, ar, ac, ac], -1)
    cos = np.cos(ang).astype(np.float32).T
    sin = np.sin(ang).astype(np.float32).T.copy()
    sin[0:32] *= -1.0
    sin[64:96] *= -1.0
    return np.ascontiguousarray(cos), np.ascontiguousarray(sin)


def make_perm():
    P = np.zeros((128, 128), np.float32)
    for m in range(128):
        k = m + 32 if (m % 64) < 32 else m - 32
        P[k, m] = 1.0
    return P


_NC_CACHE = {}


def run_cores(core_x, conns, weights, HALF):
    if HALF not in _NC_CACHE:
        _NC_CACHE[HALF] = build(HALF)
    nc = _NC_CACHE[HALF]
    perm = make_perm()
    consts = {}
    for conn in set(conns):
        cos, sin = make_rope(HALF, conn)
        consts[conn] = dict(cosT=cos, sinT=sin, etab=make_etab(HALF, conn),
                            cross=np.full((128, 1), 1.0 if conn else 0.0, np.float32))
    in_maps = []
    for x, conn in zip(core_x, conns):
        m = dict(weights)
        m["xT"] = np.ascontiguousarray(x.T)
        m["permM"] = perm
        m.update(consts[conn])
        in_maps.append(m)
    res = run_bass_kernel_spmd(nc, in_maps, core_ids=list(range(len(in_maps))))
    return [np.ascontiguousarray(r["yT"].T) for r in res.results]


def kernel(x_prompt, x_sample, w_in, b_in, q_gain, k_gain, w_proj_a, w_proj_b, w_out, ln_g, ln_b):
    f = lambda a: np.ascontiguousarray(np.asarray(a, dtype=np.float32))
    x_prompt, x_sample = f(x_prompt), f(x_sample)
    weights = dict(w_in=f(w_in), b_in=f(b_in), q_gain=f(q_gain), k_gain=f(k_gain), w_proj_a=f(w_proj_a),
                   w_proj_b=f(w_proj_b), w_out=f(w_out), ln_g=f(ln_g), ln_b=f(ln_b))
    HALF = x_prompt.shape[1]
    c0 = x_prompt[0:2].reshape(2 * HALF, D)
    c1 = x_prompt[2:4].reshape(2 * HALF, D)
    c2 = x_sample[0]
    xs = [c0, c1, c2]
    conns = [False, False, True]
    outs = run_cores(xs, conns, weights, HALF)
    y_prompt = np.concatenate([outs[0].reshape(2, HALF, D), outs[1].reshape(2, HALF, D)], 0)
    y_sample = outs[2].reshape(1, 2 * HALF, D)
    return (y_prompt.astype(np.float32), y_sample.astype(np.float32))
```

```python
import contextlib
import math
import numpy as np
import ml_dtypes
import concourse.bass as bass
import concourse.mybir as mybir
from concourse.bass_utils import run_bass_kernel_spmd

F32 = mybir.dt.float32
BF16 = mybir.dt.bfloat16
AF = mybir.ActivationFunctionType
ALU = mybir.AluOpType

D = 1024
HD = 128
INW = 9728
OFF_AQ, OFF_AK, OFF_AV, OFF_AG = 0, 1536, 3072, 4608
OFF_BQ, OFF_BK, OFF_BV, OFF_BG = 5120, 6144, 6400, 6656
OFF_GA, OFF_GB = 7680, 8704
DEPTH = 2
GRID_W = 64
A_GROUPS = ((128, 1), (512, 4), (2048, 16))
ALPHA = (2 * DEPTH) ** 0.25
RMS_EPS = 1e-6
LN_EPS = 1e-5
SCALE = 1.0 / math.sqrt(HD)
TQ = 512
PADL = 1024
PADR = 1024
NEG = -700.0
NDMA = 8


class Tl:
    __slots__ = ("t", "lw", "rd")

    def __init__(self, t):
        self.t = t
        self.lw = None
        self.rd = {}


class Sched:
    ENGS = ("pe", "act", "dve", "pool", "sp")

    def __init__(self):
        self.items = {e: [] for e in self.ENGS}
        self.cnt = {e: 0 for e in ("pe", "act", "dve", "pool")}
        self.dma_n = {"sp": 0, "pool": 0, "act": 0}
        self.dma_val = {}
        self.seen = {e: {} for e in self.ENGS}

    def _deps(self, eng, reads, writes):
        deps = {}

        def add(k, v):
            if deps.get(k, 0) < v:
                deps[k] = v
        for t in reads:
            if t.lw is not None:
                add(*t.lw)
        for t in writes:
            if t.lw is not None:
                add(*t.lw)
            for k, v in t.rd.items():
                add(k, v)
        waits = []
        for k, v in deps.items():
            if k == "pe" and eng == "pe":
                continue
            if self.seen[eng].get(k, 0) >= v:
                continue
            self.seen[eng][k] = v
            waits.append((k, v))
        return waits

    def _mark(self, me, reads, writes):
        k, v = me
        for t in reads:
            if t.rd.get(k, 0) < v:
                t.rd[k] = v
        for t in writes:
            t.lw = me
            t.rd = {}

    def op(self, eng, emit, reads=(), writes=()):
        waits = self._deps(eng, reads, writes)
        self.cnt[eng] += 1
        me = (eng, self.cnt[eng])
        self._mark(me, reads, writes)
        self.items[eng].append((waits, emit, eng, 1))

    def dma(self, q, out, in_, reads=(), writes=(), slow=False):
        n = self.dma_n[q]
        self.dma_n[q] = n + 1
        key = ("dma", q, n % NDMA)
        prev = self.dma_val.get(key, 0)
        waits = self._deps(q, reads, writes)
        if prev and self.seen[q].get(key, 0) < prev:
            self.seen[q][key] = prev
            waits.append((key, prev))
        val = prev + 16
        self.dma_val[key] = val
        self._mark((key, val), reads, writes)
        if slow:
            emit = lambda e: e.dma_start(out=out, in_=in_, allow_slow_non_contiguous=True)
        else:
            emit = lambda e: e.dma_start(out=out, in_=in_)
        self.items[q].append((waits, emit, key, 16))

    def barrier(self):
        allv = dict(self.cnt)
        allv.update(self.dma_val)
        for e in self.ENGS:
            waits = []
            for k, v in allv.items():
                if v and k != e and self.seen[e].get(k, 0) < v:
                    self.seen[e][k] = v
                    waits.append((k, v))
            if waits:
                self.items[e].append((waits, None, None, 0))


def build(HALF):
    N = 2 * HALF
    NT = N // TQ
    TPH = HALF // TQ
    nc = bass.Bass("TRN2", target_bir_lowering=False)
    S = Sched()
    es = contextlib.ExitStack()

    def dram(name, shape, dt, kind="Internal"):
        return nc.dram_tensor(name, list(shape), dt, kind=kind)

    xT = dram("xT", [D, N], F32, "ExternalInput")
    w_in = dram("w_in", [DEPTH, D, INW], F32, "ExternalInput")
    b_in = dram("b_in", [DEPTH, INW], F32, "ExternalInput")
    q_gain = dram("q_gain", [DEPTH, HD], F32, "ExternalInput")
    k_gain = dram("k_gain", [DEPTH, HD], F32, "ExternalInput")
    w_pa = dram("w_proj_a", [DEPTH, 512, D], F32, "ExternalInput")
    w_pb = dram("w_proj_b", [DEPTH, D, D], F32, "ExternalInput")
    w_o = dram("w_out", [DEPTH, D, D], F32, "ExternalInput")
    ln_g = dram("ln_g", [DEPTH, D], F32, "ExternalInput")
    ln_b = dram("ln_b", [DEPTH, D], F32, "ExternalInput")
    cosT = dram("cosT", [HD, N], F32, "ExternalInput")
    sinT = dram("sinT", [HD, N], F32, "ExternalInput")
    permM = dram("permM", [HD, HD], F32, "ExternalInput")
    cross = dram("cross", [128, 1], F32, "ExternalInput")
    NVAR = 9
    etab = dram("etab", [NVAR, 3, 4, 2, 128, TQ], BF16, "ExternalInput")
    yT = dram("yT", [D, N], F32, "ExternalOutput")
    wbf = dram("wbf", [DEPTH, INW // 128, 128, 8 * 128], BF16)
    wpabf = dram("wpabf", [DEPTH, 128, 4 * D], BF16)
    wpbbf = dram("wpbbf", [DEPTH, 128, 8 * D], BF16)
    wobf = dram("wobf", [DEPTH, 128, 8 * D], BF16)
    x1T = dram("x1T", [D, N], F32)
    NPAD = PADL + N + PADR
    akT = dram("akT", [12, 128, NPAD], BF16)
    av = dram("av", [NPAD, 1536], BF16)
    kbT = dram("kbT", [2, 128, N], BF16)
    vb = dram("vb", [N, 256], BF16)

    def sb(name, shape, dt):
        return Tl(es.enter_context(nc.sbuf_tensor(name, list(shape), dt)))

    def psum(name):
        return Tl(es.enter_context(nc.psum_tensor(name, [128, TQ], F32)))

    ones_bf = sb("ones_bf", [128, 512], BF16)
    zero_bf = sb("zero_bf", [128, 1536], BF16)
    perm_f = sb("perm_f", [128, 128], F32)
    perm_b = sb("perm_b", [128, 128], BF16)
    cross_t = sb("cross_t", [128, 1], F32)
    cross_bf = sb("cross_bf", [128, 1], BF16)
    eps_rms = sb("eps_rms", [128, 1], F32)
    eps_ln = sb("eps_ln", [128, 1], F32)
    bias_fm = sb("bias_fm", [128, INW // 128], F32)
    bias_row = sb("bias_row", [1, 1792], BF16)
    qg_t = sb("qg_t", [128, 1], F32)
    kg_t = sb("kg_t", [128, 1], F32)
    lng_t = sb("lng_t", [128, 8], F32)
    lnb_t = sb("lnb_t", [128, 8], F32)
    xf = [sb(f"xf{i}", [128, TQ], F32) for i in range(2)]
    xb = sb("xb", [128, 8, TQ], BF16)
    wstage = [sb(f"wstage{i}", [128, 8, 128], F32) for i in range(1)]
    wt = [sb(f"wt{i}", [128, 8, 128], BF16) for i in range(3)]
    wpa_t = sb("wpa_t", [128, 4, D], BF16)
    wpb_t = sb("wpb_t", [128, 8, D], BF16)
    wo_t = sb("wo_t", [128, 8, D], BF16)
    cos_t = sb("cos_t", [128, TQ], F32)
    sin_t = sb("sin_t", [128, TQ], F32)
    tA = sb("tA", [128, TQ], F32)
    tB = sb("tB", [128, TQ], F32)
    tC = sb("tC", [128, TQ], F32)
    tD = sb("tD", [128, TQ], F32)
    tbf = sb("tbf", [128, TQ], BF16)
    tbf2 = sb("tbf2", [128, TQ], BF16)
    stg = [sb(f"stg{i}", [128, TQ], BF16) for i in range(2)]
    qb = [sb(f"qb{i}", [128, TQ], BF16) for i in range(4)]
    gsil = sb("gsil", [128, TQ], F32)
    yb = sb("yb", [128, 8, TQ], BF16)
    ya = sb("ya", [128, 4, TQ], BF16)
    Pt = [sb(f"Pt{i}", [128, TQ], BF16) for i in range(3)]
    Pm = [sb(f"Pm{i}", [128, TQ], BF16) for i in range(2)]
    Et = [sb(f"Et{i}", [128, TQ], BF16) for i in range(4)]
    Kg = [sb(f"Kg{i}", [128, 2048], BF16) for i in range(2)]
    Vg = [sb(f"Vg{i}", [128, 16, 128], BF16) for i in range(2)]
    Kw = [sb(f"Kw{i}", [128, 2560], BF16) for i in range(2)]
    Vw = [sb(f"Vw{i}", [128, 32, 128], BF16) for i in range(2)]
    qa = [sb(f"qa{i}", [128, TQ], BF16) for i in range(2)]
    lrow = sb("lrow", [128, TQ], F32)
    lhi = sb("lhi", [128, TQ], BF16)
    llo = sb("llo", [128, TQ], BF16)
    lbc = sb("lbc", [128, TQ], F32)
    mrg = sb("mrg", [128, 8, TQ], BF16)
    z = sb("z", [128, 8, TQ], F32)
    yout = [sb(f"yout{i}", [128, TQ], F32) for i in range(2)]
    ps = [psum(f"ps{i}") for i in range(8)]
    PS_S = [ps[0], ps[1]]
    PS_O = [ps[2], ps[3], ps[4], ps[5]]
    PS_L = ps[6]
    PS_G = ps[7]

    def mm(out_t, out_ap, lhs_t, lhs_ap, rhs_t, rhs_ap, start, stop):
        S.op("pe", lambda e: e.matmul(out_ap, lhsT=lhs_ap, rhs=rhs_ap, start=start, stop=stop,
                                      skip_group_check=True),
             reads=[lhs_t, rhs_t] + ([] if start else [out_t]), writes=[out_t])

    def act(out_t, out_ap, in_t, in_ap, func, bias=0.0, scale=1.0, extra_reads=()):
        S.op("act", lambda e: e.activation(out=out_ap, in_=in_ap, func=func, bias=bias, scale=scale),
             reads=[in_t] + list(extra_reads), writes=[out_t])

    def tt(eng, out_t, out_ap, a_t, a_ap, b_t, b_ap, op):
        S.op(eng, lambda e: e.tensor_tensor(out=out_ap, in0=a_ap, in1=b_ap, op=op),
             reads=[a_t, b_t], writes=[out_t])

    def ts(eng, out_t, out_ap, a_t, a_ap, s1, s2, op0, op1, extra_reads=()):
        if s2 is None:
            S.op(eng, lambda e: e.tensor_scalar(out=out_ap, in0=a_ap, scalar1=s1, scalar2=None, op0=op0),
                 reads=[a_t] + list(extra_reads), writes=[out_t])
        else:
            S.op(eng, lambda e: e.tensor_scalar(out=out_ap, in0=a_ap, scalar1=s1, scalar2=s2, op0=op0, op1=op1),
                 reads=[a_t] + list(extra_reads), writes=[out_t])

    def stt(eng, out_t, out_ap, a_t, a_ap, sc, b_t, b_ap, op0, op1, extra_reads=()):
        S.op(eng, lambda e: e.scalar_tensor_tensor(out=out_ap, in0=a_ap, scalar=sc, in1=b_ap, op0=op0, op1=op1),
             reads=[a_t, b_t] + list(extra_reads), writes=[out_t])

    def cp(eng, out_t, out_ap, in_t, in_ap):
        S.op(eng, lambda e: e.tensor_copy(out=out_ap, in_=in_ap), reads=[in_t], writes=[out_t])

    def mset(eng, t, ap, val):
        S.op(eng, lambda e: e.memset(ap, val), writes=[t])

    mset("dve", ones_bf, ones_bf.t[:], 1.0)
    mset("dve", zero_bf, zero_bf.t[:], 0.0)
    mset("dve", eps_rms, eps_rms.t[:], RMS_EPS)
    mset("dve", eps_ln, eps_ln.t[:], LN_EPS)
    S.dma("sp", perm_f.t[:], permM[:, :], writes=[perm_f])
    cp("dve", perm_b, perm_b.t[:], perm_f, perm_f.t[:])
    S.dma("sp", cross_t.t[:], cross[:, :], writes=[cross_t])
    cp("dve", cross_bf, cross_bf.t[:], cross_t, cross_t.t[:])
    for h in range(12):
        S.dma("pool", akT[h, :, 0:PADL], zero_bf.t[:, 0:PADL], reads=[zero_bf])
        S.dma("pool", akT[h, :, PADL + N:NPAD], zero_bf.t[:, 0:PADR], reads=[zero_bf])
    for r0 in list(range(0, PADL, 128)) + list(range(PADL + N, NPAD, 128)):
        S.dma("pool", av[r0:r0 + 128, :], zero_bf.t[:, :], reads=[zero_bf])

    ci = 0
    for l in range(DEPTH):
        for c in range(INW // 128):
            st = wstage[0]
            w16 = wt[ci % 3]
            S.dma("sp", st.t[:], w_in[l, :, c * 128:(c + 1) * 128].rearrange("(kc p) j -> p kc j", p=128),
                  writes=[st])
            cp("dve" if ci % 2 == 0 else "pool", w16, w16.t[:], st, st.t[:])
            S.dma("pool", wbf[l, c].rearrange("p (kc j) -> p kc j", j=128), w16.t[:], reads=[w16])
            ci += 1
        for (src, dst, nk) in ((w_pa, wpabf, 4), (w_pb, wpbbf, 8), (w_o, wobf, 8)):
            for oc in range(8):
                st = wstage[0]
                w16 = wt[ci % 3]
                S.dma("sp", st.t[:, 0:nk, :],
                      src[l, :, oc * 128:(oc + 1) * 128].rearrange("(kc p) j -> p kc j", p=128), writes=[st])
                cp("dve" if ci % 2 == 0 else "pool", w16, w16.t[:, 0:nk, :], st, st.t[:, 0:nk, :])
                S.dma("pool", dst[l].rearrange("p (kc j) -> p kc j", j=D)[:, :, oc * 128:(oc + 1) * 128],
                      w16.t[:, 0:nk, :], reads=[w16])
                ci += 1
    S.barrier()
    import os
    STOP = int(os.environ.get("KSTOP", "99"))

    wcnt = [0]

    def load_w(l, c):
        w16 = wt[wcnt[0] % 3]
        wcnt[0] += 1
        S.dma("sp", w16.t[:], wbf[l, c].rearrange("p (kc j) -> p kc j", j=128), writes=[w16])
        return w16

    def proj_fm(l, c, pst):
        w16 = load_w(l, c)
        for kc in range(8):
            mm(pst, pst.t[:], w16, w16.t[:, kc, :], xb, xb.t[:, kc, :], kc == 0, kc == 7)

    xcnt = [0]

    def load_x(l, t0):
        src = xT if l == 0 else x1T
        for kc in range(8):
            xs = xf[xcnt[0] % 2]
            xcnt[0] += 1
            S.dma("sp", xs.t[:], src[kc * 128:(kc + 1) * 128, t0:t0 + TQ], writes=[xs])
            cp("pool" if kc % 2 else "dve", xb, xb.t[:, kc, :], xs, xs.t[:])

    def load_rope(t0):
        S.dma("sp", cos_t.t[:], cosT[:, t0:t0 + TQ], writes=[cos_t])
        S.dma("sp", sin_t.t[:], sinT[:, t0:t0 + TQ], writes=[sin_t])

    def norm_rope(pst, c, gain_t, out_t, out_ap):
        act(tA, tA.t[:], pst, pst.t[:], AF.Identity, bias=bias_fm.t[:, c:c + 1], extra_reads=[bias_fm])
        tt("dve", tbf, tbf.t[:], tA, tA.t[:], tA, tA.t[:], ALU.mult)
        mm(PS_G, PS_G.t[:], ones_bf, ones_bf.t[:, 0:128], tbf, tbf.t[:], True, True)
        act(tB, tB.t[:], PS_G, PS_G.t[:], AF.Sqrt, bias=eps_rms.t[:, 0:1], scale=1.0 / HD, extra_reads=[eps_rms])
        S.op("dve", lambda e: e.reciprocal(out=tB.t[:], in_=tB.t[:]), reads=[tB], writes=[tB])
        stt("dve", tC, tC.t[:], tA, tA.t[:], gain_t.t[:, 0:1], tB, tB.t[:], ALU.mult, ALU.mult,
            extra_reads=[gain_t])
        cp("dve", tbf2, tbf2.t[:], tC, tC.t[:])
        mm(PS_G, PS_G.t[:], perm_b, perm_b.t[:], tbf2, tbf2.t[:], True, True)
        tt("dve", tD, tD.t[:], PS_G, PS_G.t[:], sin_t, sin_t.t[:], ALU.mult)
        tt("dve", tC, tC.t[:], tC, tC.t[:], cos_t, cos_t.t[:], ALU.mult)
        tt("dve", out_t, out_ap, tC, tC.t[:], tD, tD.t[:], ALU.add)

    vcols = [(OFF_BV // 128 + i, "b", i) for i in range(2)] + [(OFF_AV // 128 + i, "a", i) for i in range(12)]

    for l in range(DEPTH):
        if STOP <= 2 * l:
            break
        for c in range(INW // 128):
            S.dma("sp", bias_fm.t[:, c:c + 1], b_in[l, c * 128:(c + 1) * 128].rearrange("(p o) -> p o", o=1),
                  writes=[bias_fm], slow=True)
        S.dma("sp", z.t[0:1, 0, 0:256], b_in[l:l + 1, OFF_BV:OFF_BV + 256], writes=[z])
        S.dma("sp", z.t[0:1, 1:4, :], b_in[l:l + 1, OFF_AV:OFF_AV + 1536].rearrange("o (a b) -> o a b", b=512),
              writes=[z])
        cp("dve", bias_row, bias_row.t[0:1, 0:256], z, z.t[0:1, 0, 0:256])
        cp("dve", bias_row, bias_row.t[0:1, 256:1792].rearrange("o (a b) -> o a b", b=512), z, z.t[0:1, 1:4, :])
        S.dma("sp", qg_t.t[:], q_gain[l].rearrange("(p o) -> p o", o=1), writes=[qg_t], slow=True)
        S.dma("sp", kg_t.t[:], k_gain[l].rearrange("(p o) -> p o", o=1), writes=[kg_t], slow=True)
        for c in range(8):
            S.dma("sp", lng_t.t[:, c:c + 1], ln_g[l, c * 128:(c + 1) * 128].rearrange("(p o) -> p o", o=1),
                  writes=[lng_t], slow=True)
            S.dma("sp", lnb_t.t[:, c:c + 1], ln_b[l, c * 128:(c + 1) * 128].rearrange("(p o) -> p o", o=1),
                  writes=[lnb_t], slow=True)
        S.dma("sp", wpa_t.t[:], wpabf[l].rearrange("p (kc j) -> p kc j", j=D), writes=[wpa_t])
        S.dma("sp", wpb_t.t[:], wpbbf[l].rearrange("p (kc j) -> p kc j", j=D), writes=[wpb_t])
        S.dma("sp", wo_t.t[:], wobf[l].rearrange("p (kc j) -> p kc j", j=D), writes=[wo_t])

        sc = 0
        for ti in range(NT):
            t0 = ti * TQ
            load_x(l, t0)
            load_rope(t0)
            for kvh in range(2):
                c = OFF_BK // 128 + kvh
                proj_fm(l, c, PS_S[kvh])
                so = stg[sc % 2]
                sc += 1
                norm_rope(PS_S[kvh], c, kg_t, so, so.t[:])
                S.dma("pool", kbT[kvh, :, t0:t0 + TQ], so.t[:], reads=[so])
            for h in range(12):
                c = OFF_AK // 128 + h
                pst = PS_O[h % 4]
                proj_fm(l, c, pst)
                so = stg[sc % 2]
                sc += 1
                act(so, so.t[:], pst, pst.t[:], AF.Identity, bias=bias_fm.t[:, c:c + 1], extra_reads=[bias_fm])
                S.dma("pool", akT[h, :, PADL + t0:PADL + t0 + TQ], so.t[:], reads=[so])
            for vi, (c, which, i) in enumerate(vcols):
                w16 = load_w(l, c)
                pst = PS_O[vi % 4]
                for tc in range(4):
                    for kc in range(8):
                        mm(pst, pst.t[:, tc * 128:(tc + 1) * 128], xb, xb.t[:, kc, tc * 128:(tc + 1) * 128],
                           w16, w16.t[:, kc, :], kc == 0 and tc == 0, False)
                    mm(pst, pst.t[:, tc * 128:(tc + 1) * 128], ones_bf, ones_bf.t[0:1, 0:128],
                       bias_row, bias_row.t[0:1, vi * 128:(vi + 1) * 128], False, True)
                so = stg[sc % 2]
                sc += 1
                if vi % 2 == 0:
                    act(so, so.t[:], pst, pst.t[:], AF.Identity)
                else:
                    cp("dve", so, so.t[:], pst, pst.t[:])
                if which == "b":
                    dst = vb[t0:t0 + TQ, i * 128:(i + 1) * 128]
                else:
                    dst = av[PADL + t0:PADL + t0 + TQ, i * 128:(i + 1) * 128]
                S.dma("pool", dst.rearrange("(tc p) j -> p tc j", p=128),
                      so.t[:].rearrange("p (tc j) -> p tc j", j=128), reads=[so])
        S.barrier()
        if STOP <= 2 * l + 1:
            break

        kvc = 0
        pc = 0
        awc = 0
        for ti in range(NT):
            t0 = ti * TQ
            qhalf = ti // TPH
            load_x(l, t0)
            load_rope(t0)
            for kvh in range(2):
                for a in range(4):
                    h = 4 * kvh + a
                    c = OFF_BQ // 128 + h
                    proj_fm(l, c, PS_S[a % 2])
                    norm_rope(PS_S[a % 2], c, qg_t, qb[a], qb[a].t[:])
                GK = min(2048, HALF)
                ngrp = N // GK
                cpg = GK // 128
                grp_buf = {}

                def load_grp(grp):
                    nonlocal kvc
                    kg = Kg[kvc % 2]
                    vg = Vg[kvc % 2]
                    kvc += 1
                    k0 = grp * GK
                    S.dma("sp", kg.t[:, 0:GK], kbT[kvh, :, k0:k0 + GK], writes=[kg])
                    S.dma("sp", vg.t[:, 0:cpg, :], vb[k0:k0 + GK, kvh * 128:(kvh + 1) * 128].rearrange(
                        "(m p) j -> p m j", p=128), writes=[vg])
                    is_cross = (k0 // HALF) != qhalf
                    if is_cross:
                        vflat = vg.t[:, 0:cpg, :].rearrange("p m j -> p (m j)")
                        ts("dve", vg, vflat, vg, vflat, cross_t.t[:, 0:1], 0.0, ALU.mult, ALU.add,
                           extra_reads=[cross_t])
                    grp_buf[grp] = (kg, vg, cross_bf if is_cross else ones_bf)

                units = [(grp, m, a) for grp in range(ngrp) for m in range(cpg) for a in range(4)]
                NU = len(units)
                ubuf = {}

                def issue_qk(u):
                    nonlocal pc
                    grp, m, a = units[u]
                    kg = grp_buf[grp][0]
                    pss = PS_S[pc % 2]
                    p_t = Pt[pc % 3]
                    pc += 1
                    ubuf[u] = p_t
                    mm(pss, pss.t[:], kg, kg.t[:, m * 128:(m + 1) * 128], qb[a], qb[a].t[:], True, True)
                    act(p_t, p_t.t[:], pss, pss.t[:], AF.Exp, scale=SCALE)

                def issue_pv(u):
                    grp, m, a = units[u]
                    kg, vg, lcol_t = grp_buf[grp]
                    p_t = ubuf.pop(u)
                    first = (grp == 0 and m == 0)
                    last = (grp == ngrp - 1 and m == cpg - 1)
                    mm(PS_O[a], PS_O[a].t[:], vg, vg.t[:, m, :], p_t, p_t.t[:], first, last)
                    lt_, lr_ = (PS_L, 32 * a) if a < 3 else (PS_G, 0)
                    mm(lt_, lt_.t[lr_:lr_ + 1, :], lcol_t, lcol_t.t[:, 0:1], p_t, p_t.t[:], first, last)

                load_grp(0)
                if ngrp > 1:
                    load_grp(1)
                issue_qk(0)
                issue_qk(1)
                for u in range(NU):
                    if u + 2 < NU:
                        issue_qk(u + 2)
                    issue_pv(u)
                    grp, m, a = units[u]
                    if m == cpg - 1 and a == 3 and grp + 2 < ngrp:
                        load_grp(grp + 2)
                for a in (3, 0, 1, 2):
                    h = 4 * kvh + a
                    lt_, lr_ = (PS_L, 32 * a) if a < 3 else (PS_G, 0)
                    finish_head(S, locals(), PS_O[a], lt_, lr_, yb, yb.t[:, h, :], l, OFF_BG // 128 + h)
            combos = [(hh, g) for hh in range(4) for g in range(3)]

            def prepA(ci):
                nonlocal awc
                hh, g = combos[ci]
                dil = A_GROUPS[g][1]
                ah = 4 * g + hh
                c = OFF_AQ // 128 + ah
                qt = qa[awc % 2]
                kw = Kw[awc % 2]
                vw = Vw[awc % 2]
                ets = (Et[(2 * awc) % 4], Et[(2 * awc + 1) % 4])
                awc += 1
                wlen = {1: 640, 4: 1024, 16: 2560}[dil]
                w0 = PADL + t0 - 64 * dil
                S.dma("sp", kw.t[:, 0:wlen], akT[ah, :, w0:w0 + wlen], writes=[kw])
                nq = TQ // dil
                if dil == 1:
                    S.dma("sp", vw.t[:, 0:5, :], av[w0:w0 + 640, ah * 128:(ah + 1) * 128].rearrange(
                        "(m p) j -> p m j", p=128), writes=[vw])
                    nk1 = 128
                else:
                    nk1 = 128 if dil == 4 else 32
                    for cc in range(2):
                        nk = 128 if cc == 0 else nk1
                        base = w0 + dil * 128 * cc
                        S.dma("sp", vw.t[0:nk, cc * dil:(cc + 1) * dil, :],
                              av[base:base + dil * nk, ah * 128:(ah + 1) * 128].rearrange(
                                  "(p r) j -> p r j", r=dil), writes=[vw])
                var = tile_variant(ti, TPH, NT, dil)
                for cc in range(2):
                    S.dma("sp", ets[cc].t[:], etab[var, g, hh, cc], writes=[ets[cc]])
                proj_fm(l, c, PS_G)
                if dil == 1:
                    act(qt, qt.t[:], PS_G, PS_G.t[:], AF.Identity, bias=bias_fm.t[:, c:c + 1],
                        extra_reads=[bias_fm])
                else:
                    act(qt, qt.t[:].rearrange("p (r i) -> p r i", r=dil), PS_G,
                        PS_G.t[:].rearrange("p (i r) -> p r i", r=dil), AF.Identity,
                        bias=bias_fm.t[:, c:c + 1], extra_reads=[bias_fm])
                return (qt, kw, vw, ets, nk1)

            def runA(ci, pre):
                nonlocal pc
                hh, g = combos[ci]
                dil = A_GROUPS[g][1]
                qt, kw, vw, ets, nk1 = pre
                nq = TQ // dil
                if dil == 1:
                    blocks = [(b * 128, 128, 0, b * 128) for b in range(4)]
                    vidx = lambda bi, cc: bi + cc
                else:
                    blocks = [(r * nq, nq, r, 0) for r in range(dil)]
                    vidx = lambda bi, cc, dil=dil: cc * dil + bi
                for cc in range(2):
                    nk = 128 if cc == 0 else nk1
                    pss = PS_S[pc % 2]
                    p_t = Pt[pc % 3]
                    pm_t = Pm[pc % 2]
                    e_t = ets[cc]
                    pc += 1
                    for bi, (c0, ncol, r, i0) in enumerate(blocks):
                        off = r + dil * (i0 + 128 * cc)
                        lhs = kw.t[:, off:off + (nk - 1) * dil + 1:dil] if dil > 1 else kw.t[:, off:off + nk]
                        mm(pss, pss.t[0:nk, c0:c0 + ncol], kw, lhs, qt, qt.t[:, c0:c0 + ncol], True, True)
                    act(p_t, p_t.t[0:nk, :], pss, pss.t[0:nk, :], AF.Exp, scale=SCALE)
                    tt("dve", pm_t, pm_t.t[0:nk, :], p_t, p_t.t[0:nk, :], e_t, e_t.t[0:nk, :], ALU.mult)
                    for bi, (c0, ncol, r, i0) in enumerate(blocks):
                        if dil == 1:
                            ocols = slice(c0, c0 + ncol)
                        else:
                            ocols = slice(r, r + (ncol - 1) * dil + 1, dil)
                        firstA = (g == 0 and cc == 0 and bi == 0)
                        lastA = (g == 2 and cc == 1 and bi == len(blocks) - 1)
                        mm(PS_O[0], PS_O[0].t[:, ocols], vw, vw.t[0:nk, vidx(bi, cc), :],
                           pm_t, pm_t.t[0:nk, c0:c0 + ncol], firstA, lastA)
                        mm(PS_L, PS_L.t[0:1, ocols], ones_bf, ones_bf.t[0:nk, 0:1],
                           pm_t, pm_t.t[0:nk, c0:c0 + ncol], firstA, lastA)

            preA = prepA(0)
            for ci in range(12):
                nxtA = prepA(ci + 1) if ci + 1 < 12 else None
                runA(ci, preA)
                preA = nxtA
                hh, g = combos[ci]
                if g == 2:
                    finish_head(S, locals(), PS_O[0], PS_L, 0, ya, ya.t[:, hh, :], l, OFF_AG // 128 + hh)
            for oc in range(8):
                pa = PS_O[1]
                pbp = PS_O[2]
                for kc in range(4):
                    mm(pa, pa.t[:], wpa_t, wpa_t.t[:, kc, oc * 128:(oc + 1) * 128], ya, ya.t[:, kc, :],
                       kc == 0, kc == 3)
                for kc in range(8):
                    mm(pbp, pbp.t[:], wpb_t, wpb_t.t[:, kc, oc * 128:(oc + 1) * 128], yb, yb.t[:, kc, :],
                       kc == 0, kc == 7)
                cga = OFF_GA // 128 + oc
                proj_fm(l, cga, PS_S[0])
                act(tA, tA.t[:], PS_S[0], PS_S[0].t[:], AF.Sigmoid, bias=bias_fm.t[:, cga:cga + 1],
                    extra_reads=[bias_fm])
                cgb = OFF_GB // 128 + oc
                proj_fm(l, cgb, PS_S[1])
                act(tB, tB.t[:], PS_S[1], PS_S[1].t[:], AF.Sigmoid, bias=bias_fm.t[:, cgb:cgb + 1],
                    extra_reads=[bias_fm])
                tt("dve", tC, tC.t[:], pa, pa.t[:], tA, tA.t[:], ALU.mult)
                tt("dve", tD, tD.t[:], pbp, pbp.t[:], tB, tB.t[:], ALU.mult)
                tt("dve", mrg, mrg.t[:, oc, :], tC, tC.t[:], tD, tD.t[:], ALU.add)
            src = xT if l == 0 else x1T
            for oc in range(8):
                pso = PS_O[oc % 2 + 1]
                for kc in range(8):
                    mm(pso, pso.t[:], wo_t, wo_t.t[:, kc, oc * 128:(oc + 1) * 128], mrg, mrg.t[:, kc, :],
                       kc == 0, kc == 7)
                xs = xf[xcnt[0] % 2]
                xcnt[0] += 1
                S.dma("sp", xs.t[:], src[oc * 128:(oc + 1) * 128, t0:t0 + TQ], writes=[xs])
                stt("dve", z, z.t[:, oc, :], xs, xs.t[:], ALPHA, pso, pso.t[:], ALU.mult, ALU.add)
                cp("pool", tbf if oc % 2 == 0 else tbf2, (tbf if oc % 2 == 0 else tbf2).t[:], z, z.t[:, oc, :])
                zb = tbf if oc % 2 == 0 else tbf2
                mm(PS_G, PS_G.t[:], ones_bf, ones_bf.t[:, 0:128], zb, zb.t[:], oc == 0, oc == 7)
            ts("dve", tA, tA.t[:], PS_G, PS_G.t[:], 1.0 / D, 0.0, ALU.mult, ALU.add)
            for oc in range(8):
                tt("dve", z, z.t[:, oc, :], z, z.t[:, oc, :], tA, tA.t[:], ALU.subtract)
                zb = tbf if oc % 2 == 0 else tbf2
                tt("pool", zb, zb.t[:], z, z.t[:, oc, :], z, z.t[:, oc, :], ALU.mult)
                mm(PS_S[0], PS_S[0].t[:], ones_bf, ones_bf.t[:, 0:128], zb, zb.t[:], oc == 0, oc == 7)
            act(tB, tB.t[:], PS_S[0], PS_S[0].t[:], AF.Sqrt, bias=eps_ln.t[:, 0:1], scale=1.0 / D, extra_reads=[eps_ln])
            S.op("dve", lambda e: e.reciprocal(out=tB.t[:], in_=tB.t[:]), reads=[tB], writes=[tB])
            dst = x1T if l == 0 else yT
            for oc in range(8):
                yo = yout[oc % 2]
                tt("dve", tC, tC.t[:], z, z.t[:, oc, :], tB, tB.t[:], ALU.mult)
                ts("dve", yo, yo.t[:], tC, tC.t[:], lng_t.t[:, oc:oc + 1], lnb_t.t[:, oc:oc + 1],
                   ALU.mult, ALU.add, extra_reads=[lng_t, lnb_t])
                S.dma("pool", dst[oc * 128:(oc + 1) * 128, t0:t0 + TQ], yo.t[:], reads=[yo])
        S.barrier()

    sem_names = {}
    keys = list(S.cnt.keys()) + list(S.dma_val.keys())
    for k in keys:
        nm = k if isinstance(k, str) else f"d_{k[1]}_{k[2]}"
        sem_names[k] = es.enter_context(nc.semaphore("s_" + nm))
    block = es.enter_context(nc.Block())

    def replay(engname):
        def run(e):
            for waits, emit, key, inc in S.items[engname]:
                for k, v in waits:
                    e.wait_ge(sem_names[k], v)
                if emit is not None:
                    emit(e).then_inc(sem_names[key], inc)
        return run

    block.tensor(replay("pe"))
    block.scalar(replay("act"))
    block.vector(replay("dve"))
    block.gpsimd(replay("pool"))
    block.sync(replay("sp"))
    es.close()
    return nc


def finish_head(S, env, pso, L_t, r, out_t, out_ap, l, cgate):
    PS_L, PS_G = L_t, env["PS_G"]
    lrow, lhi, llo, lbc = env["lrow"], env["lhi"], env["llo"], env["lbc"]
    ones_bf, gsil, bias_fm, tD = env["ones_bf"], env["gsil"], env["bias_fm"], env["tD"]
    act, cp, tt, mm, proj_fm = env["act"], env["cp"], env["tt"], env["mm"], env["proj_fm"]
    S.op("dve", lambda e: e.reciprocal(out=lrow.t[r:r + 1, :], in_=PS_L.t[r:r + 1, :]), reads=[PS_L], writes=[lrow])
    cp("dve", lhi, lhi.t[r:r + 1, :], lrow, lrow.t[r:r + 1, :])
    tt("dve", lrow, lrow.t[r:r + 1, :], lrow, lrow.t[r:r + 1, :], lhi, lhi.t[r:r + 1, :], ALU.subtract)
    cp("dve", llo, llo.t[r:r + 1, :], lrow, lrow.t[r:r + 1, :])
    mm(PS_G, PS_G.t[:], ones_bf, ones_bf.t[r:r + 1, 0:128], lhi, lhi.t[r:r + 1, :], True, False)
    mm(PS_G, PS_G.t[:], ones_bf, ones_bf.t[r:r + 1, 0:128], llo, llo.t[r:r + 1, :], False, True)
    act(lbc, lbc.t[:], PS_G, PS_G.t[:], AF.Identity)
    tt("dve", tD, tD.t[:], pso, pso.t[:], lbc, lbc.t[:], ALU.mult)
    proj_fm(l, cgate, PS_G)
    act(gsil, gsil.t[:], PS_G, PS_G.t[:], AF.Silu, bias=bias_fm.t[:, cgate:cgate + 1], extra_reads=[bias_fm])
    tt("dve", out_t, out_ap, tD, tD.t[:], gsil, gsil.t[:], ALU.mult)


def tile_variant(ti, TPH, NT, dil):
    half = ti // TPH
    k = ti % TPH
    if k == 0:
        v = 1
    elif k == 1:
        v = 2
    elif k == TPH - 2:
        v = 3
    elif k == TPH - 1:
        v = 4
    else:
        return 0
    return v + 4 * half


VAR_TILE = None


def make_etab(HALF, conn):
    N = 2 * HALF
    TPH = HALF // TQ
    NT = N // TQ
    slopes = (2.0 ** (-8.0 * np.arange(1, 13) / 12)).reshape(3, 4)
    rep = {}
    for ti in range(NT):
        v = tile_variant(ti, TPH, NT, 16)
        rep.setdefault(v, ti)
    out = np.zeros((9, 3, 4, 2, 128, TQ), np.float32)
    j = np.arange(128)[:, None]
    for v, ti in rep.items():
        t0 = ti * TQ
        half = ti // TPH
        lo, hi = (0, N) if conn else (half * HALF, (half + 1) * HALF)
        for g, (window, dil) in enumerate(A_GROUPS):
            nq = TQ // dil
            col = np.arange(TQ)[None, :]
            if dil == 1:
                r = np.zeros_like(col)
                i0 = (col // 128) * 128
                i = col % 128
            else:
                r = col // nq
                i0 = np.zeros_like(col)
                i = col % nq
            for cc in range(2):
                kidx = i0 - 64 + 128 * cc + j
                qidx = i0 + i
                rel = kidx - qidx
                ktok = t0 + r + dil * kidx
                valid = (np.abs(rel) <= 64) & (ktok >= lo) & (ktok < hi)
                for hh in range(4):
                    e = np.exp(-slopes[g, hh] * dil * np.abs(rel).astype(np.float64))
                    out[v, g, hh, cc] = np.where(valid, e, 0.0)
    return out.astype(ml_dtypes.bfloat16)


def make_rope(HALF, conn):
    N = 2 * HALF
    t = np.arange(N)
    p = t if conn else t % HALF
    row = (p // GRID_W).astype(np.float32)
    colp = (p % GRID_W).astype(np.float32)
    axis_dim = HD // 2
    inv = (10000.0 ** (-np.arange(0, axis_dim, 2, dtype=np.float32) / axis_dim)).astype(np.float32)
    ar = row[:, None] * inv[None]
    ac = colp[:, None] * inv[None]
    ang = np.concatenate([ar, ar, ac, ac], -1)
    cos = np.cos(ang).astype(np.float32).T
    sin = np.sin(ang).astype(np.float32).T.copy()
    sin[0:32] *= -1.0
    sin[64:96] *= -1.0
    return np.ascontiguousarray(cos), np.ascontiguousarray(sin)


def make_perm():
    P = np.zeros((128, 128), np.float32)
    for m in range(128):
        k = m + 32 if (m % 64) < 32 else m - 32
        P[k, m] = 1.0
    return P


_NC_CACHE = {}


def run_cores(core_x, conns, weights, HALF):
    if HALF not in _NC_CACHE:
        _NC_CACHE[HALF] = build(HALF)
    nc = _NC_CACHE[HALF]
    perm = make_perm()
    consts = {}
    for conn in set(conns):
        cos, sin = make_rope(HALF, conn)
        consts[conn] = dict(cosT=cos, sinT=sin, etab=make_etab(HALF, conn),
                            cross=np.full((128, 1), 1.0 if conn else 0.0, np.float32))
    in_maps = []
    for x, conn in zip(core_x, conns):
        m = dict(weights)
        m["xT"] = np.ascontiguousarray(x.T)
        m["permM"] = perm
        m.update(consts[conn])
        in_maps.append(m)
    res = run_bass_kernel_spmd(nc, in_maps, core_ids=list(range(len(in_maps))))
    return [np.ascontiguousarray(r["yT"].T) for r in res.results]


def kernel(x_prompt, x_sample, w_in, b_in, q_gain, k_gain, w_proj_a, w_proj_b, w_out, ln_g, ln_b):
    f = lambda a: np.ascontiguousarray(np.asarray(a, dtype=np.float32))
    x_prompt, x_sample = f(x_prompt), f(x_sample)
    weights = dict(w_in=f(w_in), b_in=f(b_in), q_gain=f(q_gain), k_gain=f(k_gain), w_proj_a=f(w_proj_a),
                   w_proj_b=f(w_proj_b), w_out=f(w_out), ln_g=f(ln_g), ln_b=f(ln_b))
    HALF = x_prompt.shape[1]
    c0 = x_prompt[0:2].reshape(2 * HALF, D)
    c1 = x_prompt[2:4].reshape(2 * HALF, D)
    c2 = x_sample[0]
    xs = [c0, c1, c2]
    conns = [False, False, True]
    outs = run_cores(xs, conns, weights, HALF)
    y_prompt = np.concatenate([outs[0].reshape(2, HALF, D), outs[1].reshape(2, HALF, D)], 0)
    y_sample = outs[2].reshape(1, 2 * HALF, D)
    return (y_prompt.astype(np.float32), y_sample.astype(np.float32))
```
